# Optimizing a Trainium2 kernel written in Bass

```python
import math
import jax, jax.numpy as jnp
from jax import lax
import numpy as np

D_MODEL = 2048
BATCH = 4
SEQ = 4096
DEPTH = 4

N_MIXERS = 2
N_ATTN_LAYERS = (DEPTH + 1) // 2
N_CONV_LAYERS = DEPTH // 2
N_DIFF_HEADS = 8
HEAD_DIM = 128
V_HEAD_DIM = 2 * HEAD_DIM
QK_WIDTH = 2 * N_DIFF_HEADS * HEAD_DIM
V_WIDTH = N_DIFF_HEADS * V_HEAD_DIM
ROPE_THETA = 10000.0
Q_BLOCK = 128
CONV_WIDTH = 3
D_FF = (((8 * D_MODEL + 2) // 3 + 255) // 256) * 256
RMS_EPS = 1e-6
SUBLN_EPS = 1e-5

kernel_name = "hybrid_diffattn_shortconv_swiglu"


def rms_norm(x, g, eps):
    xf = x.astype(jnp.float32)
    y = xf * lax.rsqrt(jnp.mean(xf * xf, axis=-1, keepdims=True) + eps)
    return (y * g.astype(jnp.float32)).astype(x.dtype)


def rope_tables(seq_len):
    pos = jnp.arange(seq_len, dtype=jnp.float32)
    inv_freq = 1.0 / (ROPE_THETA ** (jnp.arange(0, HEAD_DIM, 2, dtype=jnp.float32) / HEAD_DIM))
    ang = pos[:, None] * inv_freq[None, :]
    return jnp.cos(ang), jnp.sin(ang)


def apply_rope(t, cos, sin):
    tf = t.astype(jnp.float32)
    t1, t2 = jnp.split(tf, 2, axis=-1)
    c = cos[None, :, None, :]
    s = sin[None, :, None, :]
    out = jnp.concatenate([t1 * c - t2 * s, t2 * c + t1 * s], axis=-1)
    return out.astype(t.dtype)


def diff_attention(h, w_qkv, w_o, lq1, lk1, lq2, lk2, subln_g, lambda_init, cos, sin):
    b, s, _ = h.shape
    qkv = h @ w_qkv
    q, k, v = jnp.split(qkv, [QK_WIDTH, 2 * QK_WIDTH], axis=-1)
    q = apply_rope(q.reshape(b, s, 2 * N_DIFF_HEADS, HEAD_DIM), cos, sin)
    k = apply_rope(k.reshape(b, s, 2 * N_DIFF_HEADS, HEAD_DIM), cos, sin)
    v = v.reshape(b, s, N_DIFF_HEADS, V_HEAD_DIM).astype(jnp.float32)
    lam = (jnp.exp(jnp.sum(lq1.astype(jnp.float32) * lk1.astype(jnp.float32)))
           - jnp.exp(jnp.sum(lq2.astype(jnp.float32) * lk2.astype(jnp.float32)))
           + lambda_init)
    scale = HEAD_DIM ** -0.5
    n_blocks = s // Q_BLOCK
    q_blocks = q.reshape(b, n_blocks, Q_BLOCK, 2 * N_DIFF_HEADS, HEAD_DIM).transpose(1, 0, 2, 3, 4)
    starts = jnp.arange(n_blocks, dtype=jnp.int32) * Q_BLOCK
    k_pos = jnp.arange(s, dtype=jnp.int32)

    def one_block(args):
        qb, start = args
        scores = jnp.einsum('bqhd,bkhd->bhqk', qb, k).astype(jnp.float32) * scale
        q_pos = start + jnp.arange(Q_BLOCK, dtype=jnp.int32)
        causal = k_pos[None, :] <= q_pos[:, None]
        scores = jnp.where(causal[None, None], scores, -jnp.inf)
        p = jax.nn.softmax(scores, axis=-1).reshape(b, N_DIFF_HEADS, 2, Q_BLOCK, s)
        attn = p[:, :, 0] - lam * p[:, :, 1]
        return jnp.einsum('bhqk,bkhe->bqhe', attn, v)

    o = lax.map(one_block, (q_blocks, starts))
    o = o.transpose(1, 0, 2, 3, 4).reshape(b, s, N_DIFF_HEADS, V_HEAD_DIM)
    o = rms_norm(o, subln_g, SUBLN_EPS) * (1.0 - lambda_init)
    return o.reshape(b, s, V_WIDTH).astype(h.dtype) @ w_o


def short_conv_mixer(h, w_bch, conv_w, w_o):
    bch = h @ w_bch
    gate_b, gate_c, u = jnp.split(bch, 3, axis=-1)
    z = gate_c * u
    conv_filter = conv_w[:, None, :].astype(z.dtype)
    zc = lax.conv_general_dilated(
        z, conv_filter, window_strides=(1,), padding=[(CONV_WIDTH - 1, 0)],
        dimension_numbers=('NWC', 'WIO', 'NWC'), feature_group_count=D_MODEL)
    return (gate_b * zc) @ w_o


def swiglu(h, w_gate, w_up, w_down):
    return (jax.nn.silu(h @ w_gate) * (h @ w_up)) @ w_down


def setup_inputs(seed: int = 0) -> dict:
    key = jax.random.key(seed)
    ks = jax.random.split(key, 24)
    f32 = jnp.float32
    def nrm(k, shape, scale):
        return jax.random.normal(k, shape, f32) * scale
    def gain(k, shape):
        return 1.0 + 0.02 * jax.random.normal(k, shape, f32)
    res_scale = (2.0 * DEPTH) ** -0.5
    return {
        "x": nrm(ks[0], (BATCH, SEQ, D_MODEL), 1.0),
        "attn_norm_g": gain(ks[1], (N_ATTN_LAYERS, D_MODEL)),
        "w_qkv": nrm(ks[2], (N_ATTN_LAYERS, D_MODEL, 2 * QK_WIDTH + V_WIDTH), D_MODEL ** -0.5),
        "w_o_attn": nrm(ks[3], (N_ATTN_LAYERS, V_WIDTH, D_MODEL), V_WIDTH ** -0.5 * res_scale),
        "lambda_q1": nrm(ks[4], (N_ATTN_LAYERS, HEAD_DIM), 0.1),
        "lambda_k1": nrm(ks[5], (N_ATTN_LAYERS, HEAD_DIM), 0.1),
        "lambda_q2": nrm(ks[6], (N_ATTN_LAYERS, HEAD_DIM), 0.1),
        "lambda_k2": nrm(ks[7], (N_ATTN_LAYERS, HEAD_DIM), 0.1),
        "subln_g": gain(ks[8], (N_ATTN_LAYERS, V_HEAD_DIM)),
        "conv_norm_g": gain(ks[9], (N_CONV_LAYERS, D_MODEL)),
        "w_bch": nrm(ks[10], (N_CONV_LAYERS, D_MODEL, 3 * D_MODEL), D_MODEL ** -0.5),
        "conv_w": nrm(ks[11], (N_CONV_LAYERS, CONV_WIDTH, D_MODEL), CONV_WIDTH ** -0.5),
        "w_o_conv": nrm(ks[12], (N_CONV_LAYERS, D_MODEL, D_MODEL), D_MODEL ** -0.5 * res_scale),
        "ffn_norm_g": gain(ks[13], (DEPTH, D_MODEL)),
        "w_gate": nrm(ks[14], (DEPTH, D_MODEL, D_FF), D_MODEL ** -0.5),
        "w_up": nrm(ks[15], (DEPTH, D_MODEL, D_FF), D_MODEL ** -0.5),
        "w_down": nrm(ks[16], (DEPTH, D_FF, D_MODEL), D_FF ** -0.5 * res_scale),
        "final_norm_g": gain(ks[17], (D_MODEL,)),
    }


def reference(x, attn_norm_g, w_qkv, w_o_attn, lambda_q1, lambda_k1, lambda_q2, lambda_k2,
              subln_g, conv_norm_g, w_bch, conv_w, w_o_conv, ffn_norm_g, w_gate, w_up,
              w_down, final_norm_g):
    cos, sin = rope_tables(x.shape[1])
    h = x
    for i in range(DEPTH):
        j = i // N_MIXERS
        if i % N_MIXERS == 0:
            lambda_init = 0.8 - 0.6 * math.exp(-0.3 * i)
            h = h + diff_attention(rms_norm(h, attn_norm_g[j], RMS_EPS), w_qkv[j], w_o_attn[j],
                                   lambda_q1[j], lambda_k1[j], lambda_q2[j], lambda_k2[j],
                                   subln_g[j], lambda_init, cos, sin)
        else:
            h = h + short_conv_mixer(rms_norm(h, conv_norm_g[j], RMS_EPS), w_bch[j], conv_w[j],
                                     w_o_conv[j])
        h = h + swiglu(rms_norm(h, ffn_norm_g[i], RMS_EPS), w_gate[i], w_up[i], w_down[i])
    return rms_norm(h, final_norm_g, RMS_EPS)
```

```python
import math
from contextlib import ExitStack

import numpy as np
import concourse.bass as bass
import concourse.mybir as mybir
from concourse.bass_utils import run_bass_kernel_spmd

F32 = mybir.dt.float32
BF16 = mybir.dt.bfloat16
AF = mybir.ActivationFunctionType
ALU = mybir.AluOpType

RMS_EPS = 1e-6
SUBLN_EPS = 1e-5
NEG_BIG = -30000.0


class Cfg:
    def __init__(self, D=2048, DFF=5632, SL=4096, depth=4, ring=5, mode="A"):
        self.mode = mode
        self.D, self.DFF, self.SL, self.depth = D, DFF, SL, depth
        self.T = 512
        self.NT = SL // 512
        self.KD = D // 128
        self.KF = DFF // 128
        self.NH = D // 256
        self.NCOMP = 2 * self.NH
        self.CTXT = self.NT // 2
        self.ring = ring
        self.n_attn = (depth + 1) // 2
        self.n_conv = depth // 2
        if mode == "B":
            self.lo = [0 if i < 2 else self.CTXT - 1 for i in range(depth)]
            self.out_lo = self.CTXT
        else:
            self.lo = [0] * depth
            self.out_lo = 0


class DSem:
    def __init__(self, h):
        self.h, self.n = h, 0


class Phase:
    ENG = ("pe", "act", "dve", "pool", "sp")

    def __init__(self, P, name):
        self.P, self.nc, self.name = P, P.nc, name
        self.ops = {e: [] for e in self.ENG}
        self.stack = ExitStack()
        self.prog = {}
        self.seq = {}
        for e in ("pe", "act", "dve", "pool"):
            self.prog[e] = self.stack.enter_context(self.nc.semaphore(f"{name}_{e}"))
            self.seq[e] = 0
        self.dsems = []

    def dsem(self, name):
        h = self.stack.enter_context(self.nc.semaphore(f"{self.name}_{name}"))
        d = DSem(h)
        self.dsems.append(d)
        return d

    def sbuf(self, name, shape, dt):
        return self.stack.enter_context(self.nc.sbuf_tensor(f"{self.name}_{name}", shape, dt))

    def psum(self, name, shape, dt):
        return self.stack.enter_context(self.nc.psum_tensor(f"{self.name}_{name}", shape, dt))

    @staticmethod
    def _emit_waits(e, waits):
        best = {}
        for w in waits:
            if w is None:
                continue
            h, v = w
            k = id(h)
            if k not in best or best[k][1] < v:
                best[k] = (h, v)
        for h, v in best.values():
            e.wait_ge(h, v)

    def op(self, eng, f, waits=(), signal=True):
        waits = [w for w in waits if w is not None]
        tok = None
        if signal:
            self.seq[eng] += 1
            tok = (self.prog[eng], self.seq[eng])
            sem = self.prog[eng]

            def run(e, f=f, waits=waits, sem=sem):
                self._emit_waits(e, waits)
                f(e).then_inc(sem, 1)
        else:
            def run(e, f=f, waits=waits):
                self._emit_waits(e, waits)
                f(e)
        self.ops[eng].append(run)
        return tok

    def dma(self, queue, out, in_, dsem, waits=()):
        waits = [w for w in waits if w is not None]
        dsem.n += 16
        tok = (dsem.h, dsem.n)

        def run(e, out=out, in_=in_, waits=waits, h=dsem.h):
            self._emit_waits(e, waits)
            e.dma_start(out=out, in_=in_).then_inc(h, 16)
        self.ops[queue].append(run)
        return tok

    def finish(self, final_waits=()):
        fw = {e: [] for e in self.ENG}
        for e, toks in dict(final_waits).items():
            fw[e] = list(toks)
        ops = self.ops
        with self.nc.Block(self.name, no_gpsimd_drain=True) as block:
            @block.tensor
            def _(e):
                for f in ops["pe"]:
                    f(e)
                self._emit_waits(e, fw["pe"])

            @block.scalar
            def _(e):
                for f in ops["act"]:
                    f(e)
                self._emit_waits(e, fw["act"])

            @block.vector
            def _(e):
                for f in ops["dve"]:
                    f(e)
                self._emit_waits(e, fw["dve"])

            @block.gpsimd
            def _(e):
                for f in ops["pool"]:
                    f(e)
                self._emit_waits(e, fw["pool"])

            @block.sync
            def _(e):
                for f in ops["sp"]:
                    f(e)
                self._emit_waits(e, fw["sp"])
        sems = list(self.prog.values()) + [d.h for d in self.dsems]
        with self.nc.Block(self.name + "_clr", no_gpsimd_drain=True) as blk:
            @blk.gpsimd
            def _(e):
                for h in sems:
                    e.sem_clear(h)
        self.stack.close()


class Ring:
    def __init__(self, ph, nslots):
        self.ph = ph
        self.n = nslots
        self.t = [ph.sbuf(f"ring{i}", [128, 16, 512], BF16) for i in range(nslots)]
        self.ds = [ph.dsem(f"ringld{i}") for i in range(nslots)]
        self.count = 0
        self.free_tok = {}

    def load(self, src_ap, nk, extra_waits=()):
        i = self.count
        self.count += 1
        s = i % self.n
        waits = list(extra_waits)
        if i >= self.n:
            waits.append(self.free_tok[i - self.n])
        tok = self.ph.dma("sp", self.t[s][:, 0:nk, :], src_ap.rearrange("(k p) n -> p k n", p=128), self.ds[s], waits)
        return self.t[s], tok, i

    def release(self, idx, tok):
        self.free_tok[idx] = tok


class Prog:
    def __init__(self, cfg):
        self.cfg = cfg
        c = cfg
        nc = bass.Bass("TRN2", target_bir_lowering=False)
        self.nc = nc
        D, DFF, SL = c.D, c.DFF, c.SL
        na, ncv, dep = c.n_attn, c.n_conv, c.depth

        def din(name, shape, dt=F32):
            return nc.dram_tensor(name, list(shape), dt, kind="ExternalInput").ap()

        def dsc(name, shape, dt):
            return nc.dram_tensor(name, list(shape), dt).ap()

        self.x = din("x", [SL, D])
        self.w32 = {
            "w_qkv": din("w_qkv", [na, D, 3 * D]),
            "w_o_attn": din("w_o_attn", [na, D, D]),
            "w_bch": din("w_bch", [ncv, D, 3 * D]),
            "w_o_conv": din("w_o_conv", [ncv, D, D]),
            "w_gate": din("w_gate", [dep, D, DFF]),
            "w_up": din("w_up", [dep, D, DFF]),
            "w_down": din("w_down", [dep, DFF, D]),
        }
        self.wb = {k: dsc(k + "_bf", v.shape, BF16) for k, v in self.w32.items()}
        self.gains = din("gains_bc", [na + ncv + dep + 1, 128, D])
        self.lam_bc = din("lam_bc", [na, 4, 128, 128])
        self.subg_bc = din("subg_bc", [na, 128, 256])
        self.convw_t = din("convw_t", [ncv, 128, c.KD, 3])
        self.rope_cos = din("rope_cos", [SL, 512])
        self.rope_sin = din("rope_sin", [SL, 512])
        self.ident = din("ident", [128, 128])
        self.dmask = din("dmask", [128, 4, 512])
        self.ctxb = din("ctxb", [128, 1])
        self.hflag = din("hflag", [128, 1])
        self.out = nc.dram_tensor("out", [SL - c.out_lo * 512, D], F32, kind="ExternalOutput").ap()
        self.hres = dsc("hres", [SL, D], F32)
        self.qT = dsc("qT", [D, SL], BF16)
        self.kT = dsc("kT", [D, SL], BF16)
        self.v = dsc("v_sc", [SL, D], BF16)
        self.osc = dsc("o_sc", [SL, D], BF16)
        self.cv_tok = {}
        self.top = ExitStack()
        self.cv_sems = {}

    def phase_setup(self):
        c = self.cfg
        nc = self.nc
        top = self.top
        self.ident_bf = top.enter_context(nc.sbuf_tensor("ident_bf", [128, 128], BF16))
        self.dmask_bf = top.enter_context(nc.sbuf_tensor("dmask_bf", [128, 4, 512], BF16))
        self.ctxb_sb = top.enter_context(nc.sbuf_tensor("ctxb_sb", [128, 1], F32))
        self.hflag_sb = top.enter_context(nc.sbuf_tensor("hflag_sb", [128, 1], F32))
        self.eps_rms = top.enter_context(nc.sbuf_tensor("eps_rms", [128, 1], F32))
        self.eps_sub = top.enter_context(nc.sbuf_tensor("eps_sub", [128, 1], F32))
        ph = Phase(self, "setup")
        st_i = ph.sbuf("st_i", [128, 128], F32)
        st_m = ph.sbuf("st_m", [128, 4, 512], F32)
        d1, d2, d3, d4 = ph.dsem("d1"), ph.dsem("d2"), ph.dsem("d3"), ph.dsem("d4")
        t1 = ph.dma("sp", st_i[:], self.ident, d1)
        t2 = ph.dma("sp", st_m[:], self.dmask, d2)
        t3 = ph.dma("sp", self.ctxb_sb[:], self.ctxb, d3)
        t4 = ph.dma("sp", self.hflag_sb[:], self.hflag, d4)
        a = ph.op("dve", lambda e: e.tensor_copy(out=self.ident_bf[:], in_=st_i[:]), [t1])
        b = ph.op("dve", lambda e: e.tensor_copy(out=self.dmask_bf[:], in_=st_m[:]), [t2])
        cc = ph.op("dve", lambda e: e.memset(self.eps_rms[:], RMS_EPS))
        dd = ph.op("dve", lambda e: e.memset(self.eps_sub[:], SUBLN_EPS))
        self.cv_groups = []
        for i in range(c.depth):
            j = i // 2
            if i % 2 == 0:
                self.cv_groups.append([("w_qkv", j), ("w_o_attn", j)])
            else:
                self.cv_groups.append([("w_bch", j), ("w_o_conv", j)])
            self.cv_groups.append([("w_gate", i), ("w_up", i), ("w_down", i)])
        self.issue_conversions(ph)
        ph.finish({"sp": [t3, t4], "dve": [a, b, cc, dd]})

    def issue_conversions(self, ph):
        if not self.cv_groups:
            return
        grp = self.cv_groups.pop(0)
        for (wn, j) in grp:
            src = self.w32[wn][j]
            dst = self.wb[wn][j]
            R, C = src.shape
            h = self.top.enter_context(self.nc.semaphore(f"cv_{wn}_{j}"))
            ds = DSem(h)
            tok = None
            cw = max(d for d in range(128, 2049, 128) if C % d == 0)
            for r0 in range(0, R, 512):
                for c0 in range(0, C, cw):
                    tok = ph.dma("pool", dst[r0:r0 + 512, c0:c0 + cw], src[r0:r0 + 512, c0:c0 + cw], ds)
            self.cv_tok[(wn, j)] = tok

    def norm_tile(self, ph, bufs, src_rows, gbc, gtok, hT, hT_free, tile_waits=()):
        c = self.cfg
        D, KD = c.D, c.KD
        last = None
        for b in range(4):
            xs, xs_ds = bufs["xs"][b % 2], bufs["xs_ds"][b % 2]
            xn = bufs["xn"][b % 2]
            ld = ph.dma("sp", xs[:], src_rows[b * 128:(b + 1) * 128, :], xs_ds,
                        [bufs["xs_free"][b % 2]] + list(tile_waits))
            ss, lnv, rstd = bufs["ss"][b % 2], bufs["lnv"][b % 2], bufs["rstd"][b % 2]
            junk = bufs["junk"]
            t_sq = ph.op("act", lambda e, xs=xs, ss=ss: e.activation(out=junk[:], in_=xs[:], func=AF.Square, accum_out=ss[:]),
                         [ld, bufs["ss_free"][b % 2], bufs.get("junk_tok")])
            bufs["junk_tok"] = t_sq
            t_ln = ph.op("act", lambda e, ss=ss, lnv=lnv: e.activation(out=lnv[:], in_=ss[:], func=AF.Ln, bias=self.eps_rms[:], scale=1.0 / D),
                         [t_sq])
            t_rs = ph.op("act", lambda e, lnv=lnv, rstd=rstd: e.activation(out=rstd[:], in_=lnv[:], func=AF.Exp, scale=-0.5),
                         [t_ln])
            t_xn = ph.op("dve", lambda e, xs=xs, xn=xn, rstd=rstd: e.scalar_tensor_tensor(
                out=xn[:], in0=xs[:], scalar=rstd[:], in1=gbc[:], op0=ALU.mult, op1=ALU.mult),
                [t_rs, gtok, bufs["xn_free"][b % 2]])
            bufs["xs_free"][b % 2] = t_xn
            bufs["ss_free"][b % 2] = t_xn
            for half in range((KD + 7) // 8):
                k0 = half * 8
                nk = min(8, KD - k0)
                bank = bufs["tp"][bufs["tp_i"] % 2]
                bank_free = bufs["tp_free"][bufs["tp_i"] % 2]
                bi = bufs["tp_i"] % 2
                bufs["tp_i"] += 1
                tp_tok = None
                for kk in range(nk):
                    k = k0 + kk
                    w = [t_xn, bank_free] if kk == 0 else []
                    tp_tok = ph.op("pe", lambda e, xn=xn, bank=bank, k=k, kk=kk: e.transpose(
                        out=bank[:, kk * 128:(kk + 1) * 128], in_=xn[:, k * 128:(k + 1) * 128], identity=self.ident_bf[:]),
                        w, signal=(kk == nk - 1))
                ev = ph.op("act", lambda e, bank=bank, k0=k0, nk=nk, b=b: e.activation(
                    out=hT[:, k0:k0 + nk, b * 128:(b + 1) * 128],
                    in_=bank[:, 0:nk * 128].rearrange("p (k t) -> p k t", t=128), func=AF.Copy),
                    [tp_tok, hT_free])
                bufs["tp_free"][bi] = ev
                last = ev
            bufs["xn_free"][b % 2] = tp_tok
        return last

    def norm_bufs(self, ph):
        c = self.cfg
        D = c.D
        b = {}
        b["xs"] = [ph.sbuf(f"xs{i}", [128, D], F32) for i in range(2)]
        b["xs_ds"] = [ph.dsem(f"xsd{i}") for i in range(2)]
        b["xn"] = [ph.sbuf(f"xn{i}", [128, D], BF16) for i in range(2)]
        b["ss"] = [ph.sbuf(f"ss{i}", [128, 1], F32) for i in range(2)]
        b["lnv"] = [ph.sbuf(f"lnv{i}", [128, 1], F32) for i in range(2)]
        b["rstd"] = [ph.sbuf(f"rstd{i}", [128, 1], F32) for i in range(2)]
        b["junk"] = ph.sbuf("junk", [128, D], BF16)
        b["tp"] = [ph.psum(f"tp{i}", [128, 1024], BF16) for i in range(2)]
        b["tp_free"] = [None, None]
        b["tp_i"] = 0
        b["xs_free"] = [None, None]
        b["ss_free"] = [None, None]
        b["xn_free"] = [None, None]
        return b

    def load_gain(self, ph, idx):
        g = ph.sbuf("gbc", [128, self.cfg.D], F32)
        ds = ph.dsem("gbc_d")
        tok = ph.dma("sp", g[:], self.gains[idx], ds)
        return g, tok

    def proj_tm_residual(self, ph, ring, wsrc, wtok, K, lhs, lhs_tok, ps, ps_free, epi, res_src, res_dst, row0):
        c = self.cfg
        D = c.D
        kgs = [(k0, min(16, K - k0)) for k0 in range(0, K, 16)]
        last_mm = None
        stores = []
        for cg in range(D // 512):
            chunks = []
            for (k0, nk) in kgs:
                t, tok, idx = ring.load(wsrc[k0 * 128:(k0 + nk) * 128, cg * 512:(cg + 1) * 512], nk, [wtok])
                chunks.append((t, tok, idx, k0, nk))
            fin = [None] * 4
            for ci, (t, tok, idx, k0, nk) in enumerate(chunks):
                for tb in range(4):
                    for kk in range(nk):
                        k = k0 + kk
                        first = (k == 0)
                        lastk = (k == K - 1)
                        w = []
                        if kk == 0 and tb == 0:
                            w += [tok, lhs_tok]
                        if first:
                            w += [ps_free[tb]]
                        sig = lastk or (tb == 3 and kk == nk - 1)
                        mt = ph.op("pe", lambda e, t=t, tb=tb, kk=kk, k=k, first=first, lastk=lastk: e.matmul(
                            ps[tb][:], lhsT=lhs[:, k, tb * 128:(tb + 1) * 128], rhs=t[:, kk, :], start=first, stop=lastk),
                            w, signal=sig)
                        if lastk:
                            fin[tb] = mt
                        if tb == 3 and kk == nk - 1:
                            ring.release(idx, mt)
                            last_mm = mt
            for tb in range(4):
                i = epi["i"]
                epi["i"] += 1
                s = i % 4
                rp, op_ = epi["res"][s], epi["out"][s]
                rows = slice(row0 + tb * 128, row0 + (tb + 1) * 128)
                ld = ph.dma("sp", rp[:], res_src[rows, cg * 512:(cg + 1) * 512], epi["res_ds"][s], [epi["res_free"][s]])
                ad = ph.op("dve", lambda e, rp=rp, op_=op_, tb=tb: e.tensor_tensor(out=op_[:], in0=ps[tb][:], in1=rp[:], op=ALU.add),
                           [fin[tb], ld, epi["out_free"][s]])
                ps_free[tb] = ad
                epi["res_free"][s] = ad
                st = ph.dma("act", res_dst[rows, cg * 512:(cg + 1) * 512], op_[:], epi["out_ds"][s], [ad])
                epi["out_free"][s] = st
                stores.append(st)
        return last_mm, stores

    def epi_bufs(self, ph):
        e = {"i": 0}
        e["res"] = [ph.sbuf(f"resp{i}", [128, 512], F32) for i in range(4)]
        e["out"] = [ph.sbuf(f"outp{i}", [128, 512], F32) for i in range(4)]
        e["res_ds"] = [ph.dsem(f"resd{i}") for i in range(4)]
        e["out_ds"] = [ph.dsem(f"outd{i}") for i in range(4)]
        e["res_free"] = [None] * 4
        e["out_free"] = [None] * 4
        return e

    def phase_qkv(self, layer):
        c = self.cfg
        D, KD = c.D, c.KD
        j = layer // 2
        ph = Phase(self, f"qkv{layer}")
        self.issue_conversions(ph)
        ring = Ring(ph, c.ring)
        nb = self.norm_bufs(ph)
        gbc, gtok = self.load_gain(ph, j)
        hT = ph.sbuf("hT", [128, KD, 512], BF16)
        ps = [ph.psum(f"ps{i}", [128, 512], F32) for i in range(4)]
        ps_free = [None] * 4
        tq = [ph.psum(f"tq{i}", [128, 1024], BF16) for i in range(2)]
        tq_free = [None, None]
        cosb = ph.sbuf("cosb", [128, 4, 512], F32)
        sinb = ph.sbuf("sinb", [128, 4, 512], F32)
        cs_ds = ph.dsem("cs_d")
        ra = [ph.sbuf(f"ra{i}", [128, 512], F32) for i in range(2)]
        rb = [ph.sbuf(f"rb{i}", [128, 512], F32) for i in range(2)]
        rq = [ph.sbuf(f"rq{i}", [128, 512], BF16) for i in range(4)]
        rq_free = [None] * 4
        ra_free = [None, None]
        qst = [ph.sbuf(f"qst{i}", [128, 2, 512], BF16) for i in range(2)]
        qst_ds = [ph.dsem(f"qstd{i}") for i in range(2)]
        qst_free = [None, None]
        vst = [ph.sbuf(f"vst{i}", [128, 512], BF16) for i in range(4)]
        vst_ds = [ph.dsem(f"vstd{i}") for i in range(4)]
        vst_free = [None] * 4
        wsrc = self.wb["w_qkv"][j]
        wtok = self.cv_tok[("w_qkv", j)]
        src = self.x if layer == 0 else self.hres
        hT_free = None
        cs_free = None
        finals = []
        ri = 0
        qi = 0
        vi = 0
        last_rope = None
        for ti in range(0, c.NT):
            r0 = ti * 512
            hT_tok = self.norm_tile(ph, nb, src[r0:r0 + 512, :], gbc, gtok, hT, hT_free)
            cst1 = ph.dma("sp", cosb[:], self.rope_cos[r0:r0 + 512, :].rearrange("(b p) n -> p b n", p=128), cs_ds, [cs_free])
            cst2 = ph.dma("sp", sinb[:], self.rope_sin[r0:r0 + 512, :].rearrange("(b p) n -> p b n", p=128), cs_ds, [cs_free])
            last_mm = None
            for ch in range(3 * D // 512):
                kind = ch // (D // 512)
                if kind == 0 and ti < c.lo[layer]:
                    continue
                t, ltok, idx = ring.load(wsrc[:, ch * 512:(ch + 1) * 512], KD, [wtok])
                fin = [None] * 4
                for tb in range(4):
                    for k in range(KD):
                        w = []
                        if k == 0 and tb == 0:
                            w += [ltok, hT_tok]
                        if k == 0:
                            w += [ps_free[tb]]
                        sig = (k == KD - 1)
                        mt = ph.op("pe", lambda e, t=t, tb=tb, k=k: e.matmul(
                            ps[tb][:], lhsT=hT[:, k, tb * 128:(tb + 1) * 128], rhs=t[:, k, :], start=(k == 0), stop=(k == KD - 1)),
                            w, signal=sig)
                    fin[tb] = mt
                ring.release(idx, mt)
                last_mm = mt
                if kind == 2:
                    for tb in range(4):
                        s = vi % 4
                        vi += 1
                        cp = ph.op("act", lambda e, s=s, tb=tb: e.activation(out=vst[s][:], in_=ps[tb][:], func=AF.Copy),
                                   [fin[tb], vst_free[s]])
                        ps_free[tb] = cp
                        st = ph.dma("act", self.v[r0 + tb * 128:r0 + (tb + 1) * 128, (ch * 512 - 2 * D):(ch * 512 - 2 * D) + 512],
                                    vst[s][:], vst_ds[s], [cp])
                        vst_free[s] = st
                        finals.append(st)
                else:
                    rope_toks = []
                    for tb in range(4):
                        s = ri % 2
                        ri += 1
                        A, B = ra[s], rb[s]
                        pv = ps[tb][:].rearrange("p (h x) -> p h x", x=128)
                        Bv = B[:].rearrange("p (h x) -> p h x", x=128)
                        sv = sinb[:, tb, :].rearrange("p (h x) -> p h x", x=128)
                        o1 = ph.op("dve", lambda e, A=A, tb=tb: e.tensor_tensor(out=A[:], in0=ps[tb][:], in1=cosb[:, tb, :], op=ALU.mult),
                                   [fin[tb], cst1, cst2, ra_free[s]])
                        o2 = ph.op("dve", lambda e, pv=pv, Bv=Bv, sv=sv: e.tensor_tensor(
                            out=Bv[:, :, 0:64], in0=pv[:, :, 64:128], in1=sv[:, :, 0:64], op=ALU.mult), [o1])
                        o3 = ph.op("dve", lambda e, pv=pv, Bv=Bv, sv=sv: e.tensor_tensor(
                            out=Bv[:, :, 64:128], in0=pv[:, :, 0:64], in1=sv[:, :, 64:128], op=ALU.mult), [o2])
                        ps_free[tb] = o3
                        o4 = ph.op("dve", lambda e, A=A, B=B, tb=tb: e.tensor_tensor(out=rq[tb][:], in0=A[:], in1=B[:], op=ALU.add),
                                   [o3, rq_free[tb]])
                        ra_free[s] = o4
                        rope_toks.append(o4)
                        last_rope = o4
                    dstT = self.qT if kind == 0 else self.kT
                    head0 = (ch % (D // 512)) * 4
                    for hp in range(2):
                        bi = qi % 2
                        qi += 1
                        bank = tq[bi]
                        tt = None
                        for hh in range(2):
                            hd = hp * 2 + hh
                            for tb in range(4):
                                w = [rope_toks[tb]]
                                if hh == 0 and tb == 0:
                                    w.append(tq_free[bi])
                                lastt = (hh == 1 and tb == 3)
                                tt = ph.op("pe", lambda e, bank=bank, hh=hh, tb=tb, hd=hd: e.transpose(
                                    out=bank[:, hh * 512 + tb * 128: hh * 512 + (tb + 1) * 128],
                                    in_=rq[tb][:, hd * 128:(hd + 1) * 128], identity=self.ident_bf[:]),
                                    w, signal=lastt or (hp == 1 and hh == 1))
                                if hp == 1 and hh == 1:
                                    rq_free[tb] = tt
                        ev = ph.op("act", lambda e, bank=bank, bi=bi: e.activation(
                            out=qst[bi][:].rearrange("p a t -> p (a t)"), in_=bank[:], func=AF.Copy),
                            [tt, qst_free[bi]])
                        tq_free[bi] = ev
                        rows = slice((head0 + hp * 2) * 128, (head0 + hp * 2 + 2) * 128)
                        st = ph.dma("act", dstT[rows, r0:r0 + 512].rearrange("(a p) t -> p a t", p=128), qst[bi][:], qst_ds[bi], [ev])
                        qst_free[bi] = st
                        finals.append(st)
            hT_free = last_mm
            cs_free = last_rope
        ph.finish({"act": finals[-40:], "sp": [], "dve": []})

    def phase_att(self, layer):
        c = self.cfg
        D, SL, NT = c.D, c.SL, c.NT
        j = layer // 2
        lam_init = 0.8 - 0.6 * math.exp(-0.3 * layer)
        scale = 128 ** -0.5
        ph = Phase(self, f"att{layer}")
        self.issue_conversions(ph)
        NKB = SL // 128
        KT = [[ph.sbuf(f"KT{b}_{cc}", [128, SL], BF16) for cc in range(2)] for b in range(2)]
        KT_ds = [[ph.dsem(f"KTd{b}_{cc}") for cc in range(2)] for b in range(2)]
        VA = [ph.sbuf(f"VA{b}", [128, NKB, 258], BF16) for b in range(2)]
        VA_ds = [ph.dsem(f"VAd{b}") for b in range(2)]
        QT = [[ph.sbuf(f"QT{b}_{cc}", [128, 512], BF16) for cc in range(2)] for b in range(2)]
        QT_ds = [[ph.dsem(f"QTd{b}_{cc}") for cc in range(2)] for b in range(2)]
        PT = [ph.sbuf(f"PT{i}", [128, 512], BF16) for i in range(3)]
        psS = [ph.psum(f"psS{i}", [128, 512], F32) for i in range(2)]
        psO = [ph.psum(f"psO{i}", [128, 512], F32) for i in range(4)]
        t1 = [ph.sbuf(f"t1_{i}", [128, 256], F32) for i in range(4)]
        ocb = [ph.sbuf(f"ocb{i}", [128, 256], F32) for i in range(2)]
        obf = [ph.sbuf(f"obf{i}", [128, 256], BF16) for i in range(4)]
        obf_ds = [ph.dsem(f"obfd{i}") for i in range(4)]
        sm = {n: [ph.sbuf(f"{n}{i}", [128, 1], F32) for i in range(2)] for n in ("r1", "r2", "ss", "lnv", "rs")}
        junk = ph.sbuf("junk", [128, 256], BF16)
        lt = [ph.sbuf(f"lt{i}", [128, 128], F32) for i in range(4)]
        lds = [ph.dsem(f"ltd{i}") for i in range(4)]
        lj = ph.sbuf("lj", [128, 128], F32)
        ls = [ph.sbuf(f"ls{i}", [128, 1], F32) for i in range(2)]
        le = [ph.sbuf(f"le{i}", [128, 1], F32) for i in range(2)]
        neglam = ph.sbuf("neglam", [128, 1], F32)
        subg = ph.sbuf("subg", [128, 256], F32)
        sg_ds = ph.dsem("sgd")
        ltok = [ph.dma("sp", lt[i][:], self.lam_bc[j, i], lds[i]) for i in range(4)]
        sgtok = ph.dma("sp", subg[:], self.subg_bc[j], sg_ds)
        a1 = ph.op("dve", lambda e: e.scalar_tensor_tensor(out=lj[:], in0=lt[0][:], scalar=1.0, in1=lt[1][:],
                                                            op0=ALU.mult, op1=ALU.mult, accum_out=ls[0][:]), [ltok[0], ltok[1]])
        a2 = ph.op("dve", lambda e: e.scalar_tensor_tensor(out=lj[:], in0=lt[2][:], scalar=1.0, in1=lt[3][:],
                                                            op0=ALU.mult, op1=ALU.mult, accum_out=ls[1][:]), [ltok[2], ltok[3], a1])
        e1 = ph.op("act", lambda e: e.activation(out=le[0][:], in_=ls[0][:], func=AF.Exp), [a1])
        e2 = ph.op("act", lambda e: e.activation(out=le[1][:], in_=ls[1][:], func=AF.Exp), [a2])
        a3 = ph.op("dve", lambda e: e.tensor_tensor(out=neglam[:], in0=le[1][:], in1=le[0][:], op=ALU.subtract), [e1, e2])
        a4 = ph.op("dve", lambda e: e.tensor_scalar(out=neglam[:], in0=neglam[:], scalar1=-lam_init, scalar2=None, op0=ALU.add), [a3])
        a5 = ph.op("dve", lambda e: e.tensor_scalar(out=subg[:], in0=subg[:], scalar1=(1.0 - lam_init), scalar2=None, op0=ALU.mult), [sgtok])
        ones_tok = [ph.op("dve", lambda e, b=b: e.memset(VA[b][:, :, 256:257], 1.0)) for b in range(2)]
        const_tok = [a4, a5]

        KT_free = [[None, None], [None, None]]
        VA_free = [None, None]
        QT_free = [[None, None], [None, None]]
        PT_free = [None] * 3
        psS_free = [None, None]
        psO_free = [None] * 4
        t1_free = [None] * 4
        ocb_free = [None, None]
        obf_free = [None] * 4
        sm_free = [None, None]
        junk_tok = None
        finals = []
        si = 0
        qi = 0
        oi = 0
        ei = 0
        for h in range(c.NH):
            hb = h % 2
            ktok = []
            for cc in range(2):
                comp = 2 * h + cc
                ktok.append(ph.dma("sp", KT[hb][cc][:], self.kT[comp * 128:(comp + 1) * 128, :], KT_ds[hb][cc], [KT_free[hb][cc]]))
            vtok = ph.dma("sp", VA[hb][:, :, 0:256], self.v[:, h * 256:(h + 1) * 256].rearrange("(b p) e -> p b e", p=128),
                          VA_ds[hb], [VA_free[hb], ones_tok[hb]])
            last_pv_head = None
            last_s_head = [None, None]
            for Q in range(c.lo[layer], NT):
                qb_ = qi % 2
                qi += 1
                qtok = []
                for cc in range(2):
                    comp = 2 * h + cc
                    qtok.append(ph.dma("sp", QT[qb_][cc][:], self.qT[comp * 128:(comp + 1) * 128, Q * 512:(Q + 1) * 512],
                                       QT_ds[qb_][cc], [QT_free[qb_][cc]]))
                for cc in range(2):
                    nkb = 4 * Q + 4
                    s_toks = {}
                    p_toks = {}

                    def emit_S(kb):
                        nonlocal si
                        sb_ = si % 2
                        si += 1
                        w = [psS_free[sb_]]
                        if kb == 0:
                            w += [ktok[cc], qtok[cc]]
                        tk = ph.op("pe", lambda e, sb_=sb_, kb=kb, kt=KT[hb][cc], qt=QT[qb_][cc]: e.matmul(
                            psS[sb_][:], lhsT=kt[:, kb * 128:(kb + 1) * 128], rhs=qt[:], start=True, stop=True), w)
                        s_toks[kb] = (tk, sb_)
                        return tk

                    def emit_exp(kb):
                        nonlocal ei
                        tk, sb_ = s_toks[kb]
                        pb = ei % 3
                        ei += 1
                        use_ctx = (Q >= c.CTXT and kb < c.CTXT * 4)
                        if use_ctx:
                            f = lambda e, sb_=sb_, pb=pb: e.activation(out=PT[pb][:], in_=psS[sb_][:], func=AF.Exp,
                                                                        bias=self.ctxb_sb[:], scale=scale)
                        else:
                            f = lambda e, sb_=sb_, pb=pb: e.activation(out=PT[pb][:], in_=psS[sb_][:], func=AF.Exp, scale=scale)
                        ex = ph.op("act", f, [tk, PT_free[pb]])
                        psS_free[sb_] = ex
                        v = kb - 4 * Q
                        if v >= 0:
                            ex = ph.op("dve", lambda e, pb=pb, v=v: e.tensor_tensor(
                                out=PT[pb][:], in0=PT[pb][:], in1=self.dmask_bf[:, v, :], op=ALU.mult), [ex])
                        p_toks[kb] = (ex, pb)

                    def emit_PV(kb):
                        ex, pb = p_toks[kb]
                        v = kb - 4 * Q
                        lastt = None
                        for jq in range(4):
                            if v >= 0 and jq < v:
                                continue
                            w = []
                            if lastt is None:
                                w += [ex]
                                if kb == 0:
                                    w += [vtok]
                            if kb == 0:
                                w += [psO_free[jq]]
                            stop = (kb == 4 * Q + jq)
                            lastt = ph.op("pe", lambda e, jq=jq, pb=pb, kb=kb, stop=stop, va=VA[hb]: e.matmul(
                                psO[jq][:, 0:257], lhsT=PT[pb][:, jq * 128:(jq + 1) * 128], rhs=va[:, kb, 0:257],
                                start=(kb == 0), stop=stop), w, signal=(stop or jq == 3))
                            if stop:
                                o_fin[jq] = lastt
                        PT_free[pb] = lastt
                        return lastt

                    o_fin = [None] * 4
                    emit_S(0)
                    lastpv = None
                    for kb in range(nkb):
                        if kb + 1 < nkb:
                            emit_S(kb + 1)
                        emit_exp(kb)
                        lastpv = emit_PV(kb)
                    last_pv_head = lastpv
                    last_s_head[cc] = s_toks[nkb - 1][0]
                    for jq in range(4):
                        m = oi % 2
                        if cc == 0:
                            r = ph.op("dve", lambda e, jq=jq, m=m: e.reciprocal(out=sm["r1"][m][:], in_=psO[jq][:, 256:257]),
                                      [o_fin[jq], sm_free[m]])
                            tt = ph.op("dve", lambda e, jq=jq, m=m: e.tensor_scalar(
                                out=t1[jq][:], in0=psO[jq][:, 0:256], scalar1=sm["r1"][m][:], scalar2=None, op0=ALU.mult),
                                [r, t1_free[jq]])
                            psO_free[jq] = tt
                            sm_free[m] = tt
                            oi += 1
                        else:
                            r = ph.op("dve", lambda e, jq=jq, m=m: e.reciprocal(out=sm["r2"][m][:], in_=psO[jq][:, 256:257]),
                                      [o_fin[jq], sm_free[m]] + const_tok)
                            r2 = ph.op("dve", lambda e, m=m: e.tensor_tensor(out=sm["r2"][m][:], in0=sm["r2"][m][:], in1=neglam[:], op=ALU.mult), [r])
                            oc = ph.op("dve", lambda e, jq=jq, m=m: e.scalar_tensor_tensor(
                                out=ocb[m][:], in0=psO[jq][:, 0:256], scalar=sm["r2"][m][:], in1=t1[jq][:], op0=ALU.mult, op1=ALU.add),
                                [r2, ocb_free[m]])
                            psO_free[jq] = oc
                            t1_free[jq] = oc
                            sq = ph.op("act", lambda e, m=m: e.activation(out=junk[:], in_=ocb[m][:], func=AF.Square, accum_out=sm["ss"][m][:]),
                                       [oc, junk_tok])
                            junk_tok = sq
                            ln = ph.op("act", lambda e, m=m: e.activation(out=sm["lnv"][m][:], in_=sm["ss"][m][:], func=AF.Ln,
                                                                           bias=self.eps_sub[:], scale=1.0 / 256), [sq])
                            rs = ph.op("act", lambda e, m=m: e.activation(out=sm["rs"][m][:], in_=sm["lnv"][m][:], func=AF.Exp, scale=-0.5), [ln])
                            ob = oi % 4
                            fo = ph.op("dve", lambda e, m=m, ob=ob: e.scalar_tensor_tensor(
                                out=obf[ob][:], in0=ocb[m][:], scalar=sm["rs"][m][:], in1=subg[:], op0=ALU.mult, op1=ALU.mult),
                                [rs, obf_free[ob]])
                            ocb_free[m] = fo
                            sm_free[m] = fo
                            rows = slice(Q * 512 + jq * 128, Q * 512 + (jq + 1) * 128)
                            st = ph.dma("act", self.osc[rows, h * 256:(h + 1) * 256], obf[ob][:], obf_ds[ob], [fo])
                            obf_free[ob] = st
                            finals.append(st)
                            oi += 1
                    QT_free[qb_][cc] = s_toks[nkb - 1][0]
            for cc in range(2):
                KT_free[hb][cc] = last_s_head[cc]
            VA_free[hb] = last_pv_head
        ph.finish({"act": finals[-8:]})

    def phase_oproj(self, layer):
        c = self.cfg
        D, KD = c.D, c.KD
        j = layer // 2
        ph = Phase(self, f"opj{layer}")
        self.issue_conversions(ph)
        ring = Ring(ph, c.ring)
        epi = self.epi_bufs(ph)
        oT = ph.sbuf("oT", [128, KD, 512], BF16)
        ob = [ph.sbuf(f"ob{i}", [128, D], BF16) for i in range(2)]
        ob_ds = [ph.dsem(f"obd{i}") for i in range(2)]
        ob_free = [None, None]
        tp = [ph.psum(f"tp{i}", [128, 1024], BF16) for i in range(2)]
        tp_free = [None, None]
        ps = [ph.psum(f"ps{i}", [128, 512], F32) for i in range(4)]
        ps_free = [None] * 4
        oT_free = None
        res_src = self.x if layer == 0 else self.hres
        allst = []
        tpi = 0
        for ti in range(c.lo[layer], c.NT):
            r0 = ti * 512
            last_ev = None
            for b in range(4):
                s = b % 2
                ld = ph.dma("sp", ob[s][:], self.osc[r0 + b * 128:r0 + (b + 1) * 128, :], ob_ds[s], [ob_free[s]])
                tt = None
                for half in range((KD + 7) // 8):
                    k0 = half * 8
                    nk = min(8, KD - k0)
                    bi = tpi % 2
                    tpi += 1
                    for kk in range(nk):
                        k = k0 + kk
                        w = [ld, tp_free[bi]] if kk == 0 else []
                        tt = ph.op("pe", lambda e, s=s, bi=bi, k=k, kk=kk: e.transpose(
                            out=tp[bi][:, kk * 128:(kk + 1) * 128], in_=ob[s][:, k * 128:(k + 1) * 128], identity=self.ident_bf[:]),
                            w, signal=(kk == nk - 1))
                    ev = ph.op("act", lambda e, bi=bi, k0=k0, nk=nk, b=b: e.activation(
                        out=oT[:, k0:k0 + nk, b * 128:(b + 1) * 128],
                        in_=tp[bi][:, 0:nk * 128].rearrange("p (k t) -> p k t", t=128), func=AF.Copy), [tt, oT_free])
                    tp_free[bi] = ev
                    last_ev = ev
                ob_free[s] = tt
            last_mm, sts = self.proj_tm_residual(ph, ring, self.wb["w_o_attn"][j], self.cv_tok[("w_o_attn", j)], KD,
                                                 oT, last_ev, ps, ps_free, epi, res_src, self.hres, r0)
            oT_free = last_mm
            allst += sts
        ph.finish({"act": allst[-8:]})

    def phase_ffn(self, layer):
        c = self.cfg
        D, KD, KF = c.D, c.KD, c.KF
        ph = Phase(self, f"ffn{layer}")
        self.issue_conversions(ph)
        ring = Ring(ph, c.ring)
        nb = self.norm_bufs(ph)
        epi = self.epi_bufs(ph)
        gbc, gtok = self.load_gain(ph, c.n_attn + c.n_conv + layer)
        hT = ph.sbuf("hT", [128, KD, 512], BF16)
        actT = ph.sbuf("actT", [128, KF, 512], BF16)
        pg = [ph.psum(f"pg{i}", [128, 512], F32) for i in range(3)]
        pu = [ph.psum(f"pu{i}", [128, 512], F32) for i in range(3)]
        pp_free = [None] * 3
        sg = [ph.sbuf(f"sg{i}", [128, 512], F32) for i in range(2)]
        sg_free = [None, None]
        ps_free = [None] * 4
        hT_free = None
        actT_free = None
        wg, wu, wd = self.wb["w_gate"][layer], self.wb["w_up"][layer], self.wb["w_down"][layer]
        tg, tu, td = self.cv_tok[("w_gate", layer)], self.cv_tok[("w_up", layer)], self.cv_tok[("w_down", layer)]
        allst = []
        pi = 0
        for ti in range(c.lo[layer], c.NT):
            r0 = ti * 512
            hT_tok = self.norm_tile(ph, nb, self.hres[r0:r0 + 512, :], gbc, gtok, hT, hT_free)
            last_mm = None
            last_act = None
            for i in range(c.DFF // 512):
                gt, gl, gi = ring.load(wg[:, i * 512:(i + 1) * 512], KD, [tg])
                ut, ul, ui = ring.load(wu[:, i * 512:(i + 1) * 512], KD, [tu])
                for m in range(4):
                    p = pi % 3
                    pi += 1
                    mg = None
                    for k in range(KD):
                        w = []
                        if k == 0:
                            w += [pp_free[p]]
                            if m == 0:
                                w += [gl, hT_tok]
                        mg = ph.op("pe", lambda e, p=p, gt=gt, m=m, k=k: e.matmul(
                            pg[p][:], lhsT=gt[:, k, m * 128:(m + 1) * 128], rhs=hT[:, k, :], start=(k == 0), stop=(k == KD - 1)),
                            w, signal=(k == KD - 1))
                    mu = None
                    for k in range(KD):
                        w = [ul] if (k == 0 and m == 0) else []
                        mu = ph.op("pe", lambda e, p=p, ut=ut, m=m, k=k: e.matmul(
                            pu[p][:], lhsT=ut[:, k, m * 128:(m + 1) * 128], rhs=hT[:, k, :], start=(k == 0), stop=(k == KD - 1)),
                            w, signal=(k == KD - 1))
                    s = pi % 2
                    sl = ph.op("act", lambda e, p=p, s=s: e.activation(out=sg[s][:], in_=pg[p][:], func=AF.Silu), [mg, sg_free[s]])
                    ml = ph.op("dve", lambda e, p=p, s=s, i=i, m=m: e.tensor_tensor(
                        out=actT[:, i * 4 + m, :], in0=sg[s][:], in1=pu[p][:], op=ALU.mult), [sl, mu, actT_free])
                    sg_free[s] = ml
                    pp_free[p] = ml
                    last_act = ml
                ring.release(gi, mu)
                ring.release(ui, mu)
                last_mm = mu
            hT_free = last_mm
            ps = [pg[0], pg[1], pu[0], pu[1]]
            ps_free = [last_act] * 4
            lmm, sts = self.proj_tm_residual(ph, ring, wd, td, KF, actT, last_act, ps, ps_free, epi, self.hres, self.hres, r0)
            actT_free = lmm
            pp_free = [ps_free[3]] * 3
            allst += sts
        ph.finish({"act": allst[-8:]})

    def phase_conv(self, layer):
        c = self.cfg
        D, KD = c.D, c.KD
        j = layer // 2
        ph = Phase(self, f"cnv{layer}")
        self.issue_conversions(ph)
        ring = Ring(ph, c.ring)
        nb = self.norm_bufs(ph)
        epi = self.epi_bufs(ph)
        gbc, gtok = self.load_gain(ph, c.n_attn + j)
        hT = ph.sbuf("hT", [128, KD, 512], BF16)
        yT = ph.sbuf("yT", [128, KD, 512], BF16)
        cw = ph.sbuf("cw", [128, KD, 3], F32)
        cw_ds = ph.dsem("cwd")
        cwtok = ph.dma("sp", cw[:], self.convw_t[j], cw_ds)
        halo = ph.sbuf("halo", [128, KD, 2], F32)
        hz = ph.op("dve", lambda e: e.memset(halo[:], 0.0))
        pb_ = [ph.psum(f"pb{i}", [128, 512], F32) for i in range(2)]
        pc_ = [ph.psum(f"pc{i}", [128, 512], F32) for i in range(2)]
        pu_ = [ph.psum(f"pu{i}", [128, 512], F32) for i in range(2)]
        pp_free = [None, None]
        gcs = [ph.sbuf(f"gcs{i}", [128, 512], F32) for i in range(2)]
        zb = [ph.sbuf(f"zb{i}", [128, 514], F32) for i in range(2)]
        zc = [ph.sbuf(f"zc{i}", [128, 512], F32) for i in range(2)]
        buf_free = [None, None]
        wsrc = self.wb["w_bch"][j]
        wtok = self.cv_tok[("w_bch", j)]
        ps_free = [None] * 4
        hT_free = None
        yT_free = None
        halo_tok = {m: hz for m in range(KD)}
        allst = []
        pi = 0
        nD = D // 512
        for ti in range(c.lo[layer], c.NT):
            r0 = ti * 512
            hT_tok = self.norm_tile(ph, nb, self.hres[r0:r0 + 512, :], gbc, gtok, hT, hT_free)
            if ti == c.CTXT:
                hf = ph.op("dve", lambda e: e.tensor_scalar(out=halo[:], in0=halo[:], scalar1=self.hflag_sb[:], scalar2=None, op0=ALU.mult),
                           list(halo_tok.values()))
                halo_tok = {m: hf for m in range(KD)}
            last_mm = None
            last_y = None
            for i in range(nD):
                bt, bl, bi_ = ring.load(wsrc[:, i * 512:(i + 1) * 512], KD, [wtok])
                ct, cl, ci_ = ring.load(wsrc[:, D + i * 512:D + (i + 1) * 512], KD, [wtok])
                ut, ul, ui_ = ring.load(wsrc[:, 2 * D + i * 512:2 * D + (i + 1) * 512], KD, [wtok])
                for m in range(4):
                    p = pi % 2
                    pi += 1
                    mm = {}
                    for nm, wt, lt_, pst in (("b", bt, bl, pb_), ("c", ct, cl, pc_), ("u", ut, ul, pu_)):
                        for k in range(KD):
                            w = []
                            if k == 0 and nm == "b":
                                w += [pp_free[p]]
                            if k == 0 and m == 0:
                                w += [lt_, hT_tok]
                            mm[nm] = ph.op("pe", lambda e, pst=pst, p=p, wt=wt, m=m, k=k: e.matmul(
                                pst[p][:], lhsT=wt[:, k, m * 128:(m + 1) * 128], rhs=hT[:, k, :], start=(k == 0), stop=(k == KD - 1)),
                                w, signal=(k == KD - 1))
                    fm = i * 4 + m
                    g1 = ph.op("act", lambda e, p=p: e.activation(out=gcs[p][:], in_=pc_[p][:], func=AF.Copy), [mm["c"], buf_free[p]])
                    z1 = ph.op("dve", lambda e, p=p: e.tensor_tensor(out=zb[p][:, 2:514], in0=gcs[p][:], in1=pu_[p][:], op=ALU.mult),
                               [g1, mm["u"]])
                    z0 = ph.op("dve", lambda e, p=p, fm=fm: e.tensor_copy(out=zb[p][:, 0:2], in_=halo[:, fm, :]), [halo_tok[fm], z1, cwtok])
                    c1 = ph.op("dve", lambda e, p=p, fm=fm: e.tensor_scalar(
                        out=zc[p][:], in0=zb[p][:, 0:512], scalar1=cw[:, fm, 0:1], scalar2=None, op0=ALU.mult), [z0])
                    c2 = ph.op("dve", lambda e, p=p, fm=fm: e.scalar_tensor_tensor(
                        out=zc[p][:], in0=zb[p][:, 1:513], scalar=cw[:, fm, 1:2], in1=zc[p][:], op0=ALU.mult, op1=ALU.add), [c1])
                    c3 = ph.op("dve", lambda e, p=p, fm=fm: e.scalar_tensor_tensor(
                        out=zc[p][:], in0=zb[p][:, 2:514], scalar=cw[:, fm, 2:3], in1=zc[p][:], op0=ALU.mult, op1=ALU.add), [c2])
                    hn = ph.op("dve", lambda e, p=p, fm=fm: e.tensor_copy(out=halo[:, fm, :], in_=zb[p][:, 512:514]), [c3])
                    halo_tok[fm] = hn
                    y1 = ph.op("dve", lambda e, p=p, fm=fm: e.tensor_tensor(out=yT[:, fm, :], in0=zc[p][:], in1=pb_[p][:], op=ALU.mult),
                               [hn, mm["b"], yT_free])
                    pp_free[p] = y1
                    buf_free[p] = y1
                    last_y = y1
                for idx in (bi_, ci_, ui_):
                    ring.release(idx, mm["u"])
                last_mm = mm["u"]
            hT_free = last_mm
            ps = [pb_[0], pb_[1], pc_[0], pc_[1]]
            ps_free = [last_y] * 4
            lmm, sts = self.proj_tm_residual(ph, ring, self.wb["w_o_conv"][j], self.cv_tok[("w_o_conv", j)], KD,
                                             yT, last_y, ps, ps_free, epi, self.hres, self.hres, r0)
            yT_free = lmm
            pp_free = [ps_free[3], ps_free[3]]
            allst += sts
        ph.finish({"act": allst[-8:]})

    def phase_final(self):
        c = self.cfg
        D = c.D
        ph = Phase(self, "final")
        gbc, gtok = self.load_gain(ph, c.n_attn + c.n_conv + c.depth)
        xs = [ph.sbuf(f"xs{i}", [128, D], F32) for i in range(2)]
        xs_ds = [ph.dsem(f"xsd{i}") for i in range(2)]
        yo = [ph.sbuf(f"yo{i}", [128, D], F32) for i in range(2)]
        yo_ds = [ph.dsem(f"yod{i}") for i in range(2)]
        ss = [ph.sbuf(f"ss{i}", [128, 1], F32) for i in range(2)]
        lnv = [ph.sbuf(f"lnv{i}", [128, 1], F32) for i in range(2)]
        rs = [ph.sbuf(f"rs{i}", [128, 1], F32) for i in range(2)]
        junk = ph.sbuf("junk", [128, D], BF16)
        xs_free = [None, None]
        yo_free = [None, None]
        jt = None
        sts = []
        nblk = (c.NT - c.out_lo) * 4
        for b in range(nblk):
            s = b % 2
            r0 = c.out_lo * 512 + b * 128
            ld = ph.dma("sp", xs[s][:], self.hres[r0:r0 + 128, :], xs_ds[s], [xs_free[s]])
            sq = ph.op("act", lambda e, s=s: e.activation(out=junk[:], in_=xs[s][:], func=AF.Square, accum_out=ss[s][:]), [ld, jt])
            jt = sq
            ln = ph.op("act", lambda e, s=s: e.activation(out=lnv[s][:], in_=ss[s][:], func=AF.Ln, bias=self.eps_rms[:], scale=1.0 / D), [sq])
            rr = ph.op("act", lambda e, s=s: e.activation(out=rs[s][:], in_=lnv[s][:], func=AF.Exp, scale=-0.5), [ln])
            yy = ph.op("dve", lambda e, s=s: e.scalar_tensor_tensor(out=yo[s][:], in0=xs[s][:], scalar=rs[s][:], in1=gbc[:],
                                                                     op0=ALU.mult, op1=ALU.mult), [rr, gtok, yo_free[s]])
            xs_free[s] = yy
            st = ph.dma("act", self.out[b * 128:(b + 1) * 128, :], yo[s][:], yo_ds[s], [yy])
            yo_free[s] = st
            sts.append(st)
        ph.finish({"act": sts[-2:]})

    def build(self):
        c = self.cfg
        self.phase_setup()
        for i in range(c.depth):
            if i % 2 == 0:
                self.phase_qkv(i)
                self.phase_att(i)
                self.phase_oproj(i)
            else:
                self.phase_conv(i)
            self.phase_ffn(i)
        self.phase_final()
        self.top.close()
        return self.nc


def host_tables(cfg, half):
    SL, CT = cfg.SL, cfg.CTXT * 512
    pos = np.arange(SL, dtype=np.float32)
    if half == 0:
        pos = np.maximum(pos - CT, 0.0).astype(np.float32)
    inv_freq = (1.0 / (10000.0 ** (np.arange(0, 128, 2, dtype=np.float32) / 128.0))).astype(np.float32)
    ang = pos[:, None] * inv_freq[None, :]
    cos, sin = np.cos(ang).astype(np.float32), np.sin(ang).astype(np.float32)
    cosF = np.concatenate([cos, cos], axis=1)
    sinF = np.concatenate([-sin, sin], axis=1)
    rope_cos = np.ascontiguousarray(np.tile(cosF, (1, 4)))
    rope_sin = np.ascontiguousarray(np.tile(sinF, (1, 4)))
    kp = np.arange(128)[:, None, None]
    v = np.arange(4)[None, :, None]
    qf = np.arange(512)[None, None, :]
    dmask = (v * 128 + kp <= qf).astype(np.float32)
    ctxb = np.full((128, 1), 0.0 if half == 1 else NEG_BIG, np.float32)
    hflag = np.full((128, 1), 1.0 if half == 1 else 0.0, np.float32)
    return dict(rope_cos=rope_cos, rope_sin=rope_sin, dmask=dmask, ctxb=ctxb, hflag=hflag,
                ident=np.eye(128, dtype=np.float32))


def host_params(cfg, inp):
    f = np.float32
    def bc(a):
        a = np.asarray(a, f)
        return np.ascontiguousarray(np.broadcast_to(a[..., None, :], a.shape[:-1] + (128, a.shape[-1])))
    gains = np.concatenate([np.asarray(inp["attn_norm_g"], f), np.asarray(inp["conv_norm_g"], f),
                            np.asarray(inp["ffn_norm_g"], f), np.asarray(inp["final_norm_g"], f)[None]], axis=0)
    lam = np.stack([np.asarray(inp[k], f) for k in ("lambda_q1", "lambda_k1", "lambda_q2", "lambda_k2")], axis=1)
    convw = np.asarray(inp["conv_w"], f)
    ncv = convw.shape[0]
    convw_t = np.ascontiguousarray(convw.reshape(ncv, 3, cfg.KD, 128).transpose(0, 3, 2, 1))
    d = dict(gains_bc=bc(gains), lam_bc=bc(lam), subg_bc=bc(np.asarray(inp["subln_g"], f)), convw_t=convw_t)
    for k in ("w_qkv", "w_o_attn", "w_bch", "w_o_conv", "w_gate", "w_up", "w_down"):
        d[k] = np.ascontiguousarray(np.asarray(inp[k], f))
    return d


_NC_CACHE = {}


def run_model(cfg, inp, n_cores=None):
    key = (cfg.D, cfg.DFF, cfg.SL, cfg.depth, cfg.mode)
    if key not in _NC_CACHE:
        _NC_CACHE[key] = Prog(cfg).build()
    nc = _NC_CACHE[key]
    x = np.asarray(inp["x"], np.float32)
    B, S, D = x.shape
    CT = cfg.CTXT * 512
    params = host_params(cfg, inp)
    tabs = [host_tables(cfg, 0), host_tables(cfg, 1)]
    in_maps = []
    if cfg.mode == "A":
        n_cores = B if n_cores is None else n_cores
        for c in range(n_cores):
            m = dict(params)
            m.update(tabs[1])
            m["x"] = np.ascontiguousarray(x[c % B])
            in_maps.append(m)
        res = run_bass_kernel_spmd(nc, in_maps, core_ids=list(range(n_cores)))
        return np.stack([res.results[b]["out"] for b in range(B)], axis=0).astype(np.float32)
    n_cores = 2 * B if n_cores is None else n_cores
    for c in range(n_cores):
        b, half = (c // 2) % B, c % 2
        if half == 1:
            xl = x[b]
        else:
            xl = np.concatenate([np.zeros((CT, D), np.float32), x[b, :S - CT]], axis=0)
        m = dict(params)
        m.update(tabs[half])
        m["x"] = np.ascontiguousarray(xl)
        in_maps.append(m)
    res = run_bass_kernel_spmd(nc, in_maps, core_ids=list(range(n_cores)))
    out = np.zeros((B, S, D), np.float32)
    for c in range(min(n_cores, 2 * B)):
        b, half = c // 2, c % 2
        o = res.results[c]["out"]
        if half == 0:
            out[b, :S - CT] = o
        else:
            out[b, CT:] = o
    return out


def kernel(**inputs):
    cfg = Cfg()
    return run_model(cfg, inputs)
```

```python
import math
from contextlib import ExitStack

import numpy as np
import concourse.bass as bass
import concourse.mybir as mybir
from concourse.bass_utils import run_bass_kernel_spmd

F32 = mybir.dt.float32
BF16 = mybir.dt.bfloat16
AF = mybir.ActivationFunctionType
ALU = mybir.AluOpType

RMS_EPS = 1e-6
SUBLN_EPS = 1e-5
NEG_BIG = -30000.0


class Cfg:
    def __init__(self, D=2048, DFF=5632, SL=4096, depth=4, ring=5, mode="A"):
        self.mode = mode
        self.D, self.DFF, self.SL, self.depth = D, DFF, SL, depth
        self.T = 512
        self.NT = SL // 512
        self.KD = D // 128
        self.KF = DFF // 128
        self.NH = D // 256
        self.NCOMP = 2 * self.NH
        self.CTXT = self.NT // 2
        self.ring = ring
        self.n_attn = (depth + 1) // 2
        self.n_conv = depth // 2
        if mode == "B":
            self.lo = [0 if i < 2 else self.CTXT - 1 for i in range(depth)]
            self.out_lo = self.CTXT
        else:
            self.lo = [0] * depth
            self.out_lo = 0


class DSem:
    def __init__(self, h):
        self.h, self.n = h, 0


class Phase:
    ENG = ("pe", "act", "dve", "pool", "sp")

    def __init__(self, P, name):
        self.P, self.nc, self.name = P, P.nc, name
        self.ops = {e: [] for e in self.ENG}
        self.stack = ExitStack()
        self.prog = {}
        self.seq = {}
        for e in ("pe", "act", "dve", "pool"):
            self.prog[e] = self.stack.enter_context(self.nc.semaphore(f"{name}_{e}"))
            self.seq[e] = 0
        self.dsems = []

    def dsem(self, name):
        h = self.stack.enter_context(self.nc.semaphore(f"{self.name}_{name}"))
        d = DSem(h)
        self.dsems.append(d)
        return d

    def sbuf(self, name, shape, dt):
        return self.stack.enter_context(self.nc.sbuf_tensor(f"{self.name}_{name}", shape, dt))

    def psum(self, name, shape, dt):
        return self.stack.enter_context(self.nc.psum_tensor(f"{self.name}_{name}", shape, dt))

    @staticmethod
    def _emit_waits(e, waits):
        best = {}
        for w in waits:
            if w is None:
                continue
            h, v = w
            k = id(h)
            if k not in best or best[k][1] < v:
                best[k] = (h, v)
        for h, v in best.values():
            e.wait_ge(h, v)

    def op(self, eng, f, waits=(), signal=True):
        waits = [w for w in waits if w is not None]
        tok = None
        if signal:
            self.seq[eng] += 1
            tok = (self.prog[eng], self.seq[eng])
            sem = self.prog[eng]

            def run(e, f=f, waits=waits, sem=sem):
                self._emit_waits(e, waits)
                f(e).then_inc(sem, 1)
        else:
            def run(e, f=f, waits=waits):
                self._emit_waits(e, waits)
                f(e)
        self.ops[eng].append(run)
        return tok

    def dma(self, queue, out, in_, dsem, waits=()):
        waits = [w for w in waits if w is not None]
        dsem.n += 16
        tok = (dsem.h, dsem.n)

        def run(e, out=out, in_=in_, waits=waits, h=dsem.h):
            self._emit_waits(e, waits)
            e.dma_start(out=out, in_=in_).then_inc(h, 16)
        self.ops[queue].append(run)
        return tok

    def finish(self, final_waits=()):
        fw = {e: [] for e in self.ENG}
        for e, toks in dict(final_waits).items():
            fw[e] = list(toks)
        ops = self.ops
        with self.nc.Block(self.name, no_gpsimd_drain=True) as block:
            @block.tensor
            def _(e):
                for f in ops["pe"]:
                    f(e)
                self._emit_waits(e, fw["pe"])

            @block.scalar
            def _(e):
                for f in ops["act"]:
                    f(e)
                self._emit_waits(e, fw["act"])

            @block.vector
            def _(e):
                for f in ops["dve"]:
                    f(e)
                self._emit_waits(e, fw["dve"])

            @block.gpsimd
            def _(e):
                for f in ops["pool"]:
                    f(e)
                self._emit_waits(e, fw["pool"])

            @block.sync
            def _(e):
                for f in ops["sp"]:
                    f(e)
                self._emit_waits(e, fw["sp"])
        sems = list(self.prog.values()) + [d.h for d in self.dsems]
        with self.nc.Block(self.name + "_clr", no_gpsimd_drain=True) as blk:
            @blk.gpsimd
            def _(e):
                for h in sems:
                    e.sem_clear(h)
        self.stack.close()


class Ring:
    def __init__(self, ph, nslots):
        self.ph = ph
        self.n = nslots
        self.t = [ph.sbuf(f"ring{i}", [128, 16, 512], BF16) for i in range(nslots)]
        self.ds = [ph.dsem(f"ringld{i}") for i in range(nslots)]
        self.count = 0
        self.free_tok = {}

    def load(self, src_ap, nk, extra_waits=()):
        i = self.count
        self.count += 1
        s = i % self.n
        waits = list(extra_waits)
        if i >= self.n:
            waits.append(self.free_tok[i - self.n])
        tok = self.ph.dma("sp", self.t[s][:, 0:nk, :], src_ap.rearrange("(k p) n -> p k n", p=128), self.ds[s], waits)
        return self.t[s], tok, i

    def release(self, idx, tok):
        self.free_tok[idx] = tok


class Prog:
    def __init__(self, cfg):
        self.cfg = cfg
        c = cfg
        nc = bass.Bass("TRN2", target_bir_lowering=False)
        self.nc = nc
        D, DFF, SL = c.D, c.DFF, c.SL
        na, ncv, dep = c.n_attn, c.n_conv, c.depth

        def din(name, shape, dt=F32):
            return nc.dram_tensor(name, list(shape), dt, kind="ExternalInput").ap()

        def dsc(name, shape, dt):
            return nc.dram_tensor(name, list(shape), dt).ap()

        self.x = din("x", [SL, D])
        self.w32 = {
            "w_qkv": din("w_qkv", [na, D, 3 * D]),
            "w_o_attn": din("w_o_attn", [na, D, D]),
            "w_bch": din("w_bch", [ncv, D, 3 * D]),
            "w_o_conv": din("w_o_conv", [ncv, D, D]),
            "w_gate": din("w_gate", [dep, D, DFF]),
            "w_up": din("w_up", [dep, D, DFF]),
            "w_down": din("w_down", [dep, DFF, D]),
        }
        self.wb = {k: dsc(k + "_bf", v.shape, BF16) for k, v in self.w32.items()}
        self.gains = din("gains_bc", [na + ncv + dep + 1, 128, D])
        self.lam_bc = din("lam_bc", [na, 4, 128, 128])
        self.subg_bc = din("subg_bc", [na, 128, 256])
        self.convw_t = din("convw_t", [ncv, 128, c.KD, 3])
        self.rope_cos = din("rope_cos", [SL, 512])
        self.rope_sin = din("rope_sin", [SL, 512])
        self.ident = din("ident", [128, 128])
        self.dmask = din("dmask", [128, 4, 512])
        self.ctxb = din("ctxb", [128, 1])
        self.hflag = din("hflag", [128, 1])
        self.out = nc.dram_tensor("out", [SL - c.out_lo * 512, D], F32, kind="ExternalOutput").ap()
        self.hres = dsc("hres", [SL, D], F32)
        self.qT = dsc("qT", [D, SL], BF16)
        self.kT = dsc("kT", [D, SL], BF16)
        self.v = dsc("v_sc", [SL, D], BF16)
        self.osc = dsc("o_sc", [SL, D], BF16)
        self.cv_tok = {}
        self.top = ExitStack()
        self.cv_sems = {}

    def phase_setup(self):
        c = self.cfg
        nc = self.nc
        top = self.top
        self.ident_bf = top.enter_context(nc.sbuf_tensor("ident_bf", [128, 128], BF16))
        self.dmask_bf = top.enter_context(nc.sbuf_tensor("dmask_bf", [128, 4, 512], BF16))
        self.ctxb_sb = top.enter_context(nc.sbuf_tensor("ctxb_sb", [128, 1], F32))
        self.hflag_sb = top.enter_context(nc.sbuf_tensor("hflag_sb", [128, 1], F32))
        self.eps_rms = top.enter_context(nc.sbuf_tensor("eps_rms", [128, 1], F32))
        self.eps_sub = top.enter_context(nc.sbuf_tensor("eps_sub", [128, 1], F32))
        ph = Phase(self, "setup")
        st_i = ph.sbuf("st_i", [128, 128], F32)
        st_m = ph.sbuf("st_m", [128, 4, 512], F32)
        d1, d2, d3, d4 = ph.dsem("d1"), ph.dsem("d2"), ph.dsem("d3"), ph.dsem("d4")
        t1 = ph.dma("sp", st_i[:], self.ident, d1)
        t2 = ph.dma("sp", st_m[:], self.dmask, d2)
        t3 = ph.dma("sp", self.ctxb_sb[:], self.ctxb, d3)
        t4 = ph.dma("sp", self.hflag_sb[:], self.hflag, d4)
        a = ph.op("dve", lambda e: e.tensor_copy(out=self.ident_bf[:], in_=st_i[:]), [t1])
        b = ph.op("dve", lambda e: e.tensor_copy(out=self.dmask_bf[:], in_=st_m[:]), [t2])
        cc = ph.op("dve", lambda e: e.memset(self.eps_rms[:], RMS_EPS))
        dd = ph.op("dve", lambda e: e.memset(self.eps_sub[:], SUBLN_EPS))
        self.cv_groups = []
        for i in range(c.depth):
            j = i // 2
            if i % 2 == 0:
                self.cv_groups.append([("w_qkv", j), ("w_o_attn", j)])
            else:
                self.cv_groups.append([("w_bch", j), ("w_o_conv", j)])
            self.cv_groups.append([("w_gate", i), ("w_up", i), ("w_down", i)])
        self.issue_conversions(ph)
        ph.finish({"sp": [t3, t4], "dve": [a, b, cc, dd]})

    def issue_conversions(self, ph):
        if not self.cv_groups:
            return
        grp = self.cv_groups.pop(0)
        for (wn, j) in grp:
            src = self.w32[wn][j]
            dst = self.wb[wn][j]
            R, C = src.shape
            h = self.top.enter_context(self.nc.semaphore(f"cv_{wn}_{j}"))
            ds = DSem(h)
            tok = None
            cw = max(d for d in range(128, 2049, 128) if C % d == 0)
            for r0 in range(0, R, 512):
                for c0 in range(0, C, cw):
                    tok = ph.dma("pool", dst[r0:r0 + 512, c0:c0 + cw], src[r0:r0 + 512, c0:c0 + cw], ds)
            self.cv_tok[(wn, j)] = tok

    def norm_a(self, ph, bufs, src_rows, gbc, gtok):
        c = self.cfg
        D = c.D
        toks = []
        for b in range(4):
            xs, xs_ds = bufs["xs"][b % 2], bufs["xs_ds"][b % 2]
            xn = bufs["xn"][b]
            ld = ph.dma("sp", xs[:], src_rows[b * 128:(b + 1) * 128, :], xs_ds, [bufs["xs_free"][b % 2]])
            ss, lnv, rstd = bufs["ss"][b % 2], bufs["lnv"][b % 2], bufs["rstd"][b % 2]
            t_sq = ph.op("act", lambda e, xs=xs, ss=ss, xn=xn: e.activation(out=xn[:], in_=xs[:], func=AF.Square, accum_out=ss[:]),
                         [ld, bufs["ss_free"][b % 2], bufs["xn_free"][b]])
            t_ln = ph.op("act", lambda e, ss=ss, lnv=lnv: e.activation(out=lnv[:], in_=ss[:], func=AF.Ln, bias=self.eps_rms[:], scale=1.0 / D),
                         [t_sq])
            t_rs = ph.op("act", lambda e, lnv=lnv, rstd=rstd: e.activation(out=rstd[:], in_=lnv[:], func=AF.Exp, scale=-0.5),
                         [t_ln])
            t_xn = ph.op("dve", lambda e, xs=xs, xn=xn, rstd=rstd: e.scalar_tensor_tensor(
                out=xn[:], in0=xs[:], scalar=rstd[:], in1=gbc[:], op0=ALU.mult, op1=ALU.mult),
                [t_rs, gtok])
            bufs["xs_free"][b % 2] = t_xn
            bufs["ss_free"][b % 2] = t_xn
            toks.append(t_xn)
        return toks

    def norm_b(self, ph, bufs, toks, hT, hT_free):
        c = self.cfg
        KD = c.KD
        last = None
        for b in range(4):
            xn = bufs["xn"][b]
            t_xn = toks[b]
            tp_tok = None
            for half in range((KD + 7) // 8):
                k0 = half * 8
                nk = min(8, KD - k0)
                bi = bufs["tp_i"] % 2
                bank = bufs["tp"][bi]
                bank_free = bufs["tp_free"][bi]
                bufs["tp_i"] += 1
                for kk in range(nk):
                    k = k0 + kk
                    w = [t_xn, bank_free] if kk == 0 else []
                    tp_tok = ph.op("pe", lambda e, xn=xn, bank=bank, k=k, kk=kk: e.transpose(
                        out=bank[:, kk * 128:(kk + 1) * 128], in_=xn[:, k * 128:(k + 1) * 128], identity=self.ident_bf[:]),
                        w, signal=(kk == nk - 1))
                ev = ph.op("act", lambda e, bank=bank, k0=k0, nk=nk, b=b: e.activation(
                    out=hT[:, k0:k0 + nk, b * 128:(b + 1) * 128],
                    in_=bank[:, 0:nk * 128].rearrange("p (k t) -> p k t", t=128), func=AF.Copy),
                    [tp_tok, hT_free])
                bufs["tp_free"][bi] = ev
                last = ev
            bufs["xn_free"][b] = tp_tok
        return last

    def norm_tile(self, ph, bufs, src_rows, gbc, gtok, hT, hT_free, tile_waits=()):
        return self.norm_b(ph, bufs, self.norm_a(ph, bufs, src_rows, gbc, gtok), hT, hT_free)

    def norm_bufs(self, ph):
        c = self.cfg
        D = c.D
        b = {}
        b["xs"] = [ph.sbuf(f"xs{i}", [128, D], F32) for i in range(2)]
        b["xs_ds"] = [ph.dsem(f"xsd{i}") for i in range(2)]
        b["xn"] = [ph.sbuf(f"xn{i}", [128, D], BF16) for i in range(4)]
        b["ss"] = [ph.sbuf(f"ss{i}", [128, 1], F32) for i in range(2)]
        b["lnv"] = [ph.sbuf(f"lnv{i}", [128, 1], F32) for i in range(2)]
        b["rstd"] = [ph.sbuf(f"rstd{i}", [128, 1], F32) for i in range(2)]
        b["tp"] = [ph.psum(f"tp{i}", [128, 1024], BF16) for i in range(2)]
        b["tp_free"] = [None, None]
        b["tp_i"] = 0
        b["xs_free"] = [None, None]
        b["ss_free"] = [None, None]
        b["xn_free"] = [None] * 4
        return b

    def load_gain(self, ph, idx):
        g = ph.sbuf("gbc", [128, self.cfg.D], F32)
        ds = ph.dsem("gbc_d")
        tok = ph.dma("sp", g[:], self.gains[idx], ds)
        return g, tok

    def proj_tm_residual(self, ph, ring, wsrc, wtok, K, lhs, lhs_tok, ps, ps_free, epi, res_src, res_dst, row0):
        c = self.cfg
        D = c.D
        kgs = [(k0, min(16, K - k0)) for k0 in range(0, K, 16)]
        last_mm = None
        stores = []
        for cg in range(D // 512):
            chunks = []
            for (k0, nk) in kgs:
                t, tok, idx = ring.load(wsrc[k0 * 128:(k0 + nk) * 128, cg * 512:(cg + 1) * 512], nk, [wtok])
                chunks.append((t, tok, idx, k0, nk))
            fin = [None] * 4
            for ci, (t, tok, idx, k0, nk) in enumerate(chunks):
                for tb in range(4):
                    for kk in range(nk):
                        k = k0 + kk
                        first = (k == 0)
                        lastk = (k == K - 1)
                        w = []
                        if kk == 0 and tb == 0:
                            w += [tok, lhs_tok]
                        if first:
                            w += [ps_free[tb]]
                        sig = lastk or (tb == 3 and kk == nk - 1)
                        mt = ph.op("pe", lambda e, t=t, tb=tb, kk=kk, k=k, first=first, lastk=lastk: e.matmul(
                            ps[tb][:], lhsT=lhs[:, k, tb * 128:(tb + 1) * 128], rhs=t[:, kk, :], start=first, stop=lastk),
                            w, signal=sig)
                        if lastk:
                            fin[tb] = mt
                        if tb == 3 and kk == nk - 1:
                            ring.release(idx, mt)
                            last_mm = mt
            for tb in range(4):
                i = epi["i"]
                epi["i"] += 1
                s = i % 4
                rp, op_ = epi["res"][s], epi["out"][s]
                rows = slice(row0 + tb * 128, row0 + (tb + 1) * 128)
                ld = ph.dma("sp", rp[:], res_src[rows, cg * 512:(cg + 1) * 512], epi["res_ds"][s], [epi["res_free"][s]])
                ad = ph.op("dve", lambda e, rp=rp, op_=op_, tb=tb: e.tensor_tensor(out=op_[:], in0=ps[tb][:], in1=rp[:], op=ALU.add),
                           [fin[tb], ld, epi["out_free"][s]])
                ps_free[tb] = ad
                epi["res_free"][s] = ad
                st = ph.dma("act", res_dst[rows, cg * 512:(cg + 1) * 512], op_[:], epi["out_ds"][s], [ad])
                epi["out_free"][s] = st
                stores.append(st)
        return last_mm, stores

    def epi_bufs(self, ph):
        e = {"i": 0}
        e["res"] = [ph.sbuf(f"resp{i}", [128, 512], F32) for i in range(4)]
        e["out"] = [ph.sbuf(f"outp{i}", [128, 512], F32) for i in range(4)]
        e["res_ds"] = [ph.dsem(f"resd{i}") for i in range(4)]
        e["out_ds"] = [ph.dsem(f"outd{i}") for i in range(4)]
        e["res_free"] = [None] * 4
        e["out_free"] = [None] * 4
        return e

    def phase_qkv(self, layer):
        c = self.cfg
        D, KD = c.D, c.KD
        j = layer // 2
        ph = Phase(self, f"qkv{layer}")
        self.issue_conversions(ph)
        ring = Ring(ph, c.ring)
        nb = self.norm_bufs(ph)
        gbc, gtok = self.load_gain(ph, j)
        hT = ph.sbuf("hT", [128, KD, 512], BF16)
        ps = [ph.psum(f"ps{i}", [128, 512], F32) for i in range(4)]
        ps_free = [None] * 4
        tq = [ph.psum(f"tq{i}", [128, 1024], BF16) for i in range(2)]
        tq_free = [None, None]
        cosb = ph.sbuf("cosb", [128, 4, 512], F32)
        sinb = ph.sbuf("sinb", [128, 4, 512], F32)
        cs_ds = ph.dsem("cs_d")
        ra = [ph.sbuf(f"ra{i}", [128, 512], F32) for i in range(2)]
        rb = [ph.sbuf(f"rb{i}", [128, 512], F32) for i in range(2)]
        rq = [ph.sbuf(f"rq{i}", [128, 512], BF16) for i in range(4)]
        rq_free = [None] * 4
        ra_free = [None, None]
        qst = [ph.sbuf(f"qst{i}", [128, 2, 512], BF16) for i in range(2)]
        qst_ds = [ph.dsem(f"qstd{i}") for i in range(2)]
        qst_free = [None, None]
        vst = [ph.sbuf(f"vst{i}", [128, 512], BF16) for i in range(4)]
        vst_ds = [ph.dsem(f"vstd{i}") for i in range(4)]
        vst_free = [None] * 4
        wsrc = self.wb["w_qkv"][j]
        wtok = self.cv_tok[("w_qkv", j)]
        src = self.x if layer == 0 else self.hres
        hT_free = None
        cs_free = None
        finals = []
        ri = 0
        qi = 0
        vi = 0
        last_rope = None
        na_toks = self.norm_a(ph, nb, src[0:512, :], gbc, gtok)
        hT_tok = self.norm_b(ph, nb, na_toks, hT, None)
        for ti in range(0, c.NT):
            r0 = ti * 512
            if ti + 1 < c.NT:
                na_toks = self.norm_a(ph, nb, src[r0 + 512:r0 + 1024, :], gbc, gtok)
            cst1 = ph.dma("sp", cosb[:], self.rope_cos[r0:r0 + 512, :].rearrange("(b p) n -> p b n", p=128), cs_ds, [cs_free])
            cst2 = ph.dma("sp", sinb[:], self.rope_sin[r0:r0 + 512, :].rearrange("(b p) n -> p b n", p=128), cs_ds, [cs_free])
            last_mm = None
            for ch in range(3 * D // 512):
                kind = ch // (D // 512)
                if kind == 0 and ti < c.lo[layer]:
                    continue
                t, ltok, idx = ring.load(wsrc[:, ch * 512:(ch + 1) * 512], KD, [wtok])
                fin = [None] * 4
                for tb in range(4):
                    for k in range(KD):
                        w = []
                        if k == 0 and tb == 0:
                            w += [ltok, hT_tok]
                        if k == 0:
                            w += [ps_free[tb]]
                        sig = (k == KD - 1)
                        mt = ph.op("pe", lambda e, t=t, tb=tb, k=k: e.matmul(
                            ps[tb][:], lhsT=hT[:, k, tb * 128:(tb + 1) * 128], rhs=t[:, k, :], start=(k == 0), stop=(k == KD - 1)),
                            w, signal=sig)
                    fin[tb] = mt
                ring.release(idx, mt)
                last_mm = mt
                if kind == 2:
                    for tb in range(4):
                        s = vi % 4
                        vi += 1
                        cp = ph.op("act", lambda e, s=s, tb=tb: e.activation(out=vst[s][:], in_=ps[tb][:], func=AF.Copy),
                                   [fin[tb], vst_free[s]])
                        ps_free[tb] = cp
                        st = ph.dma("act", self.v[r0 + tb * 128:r0 + (tb + 1) * 128, (ch * 512 - 2 * D):(ch * 512 - 2 * D) + 512],
                                    vst[s][:], vst_ds[s], [cp])
                        vst_free[s] = st
                        finals.append(st)
                else:
                    rope_toks = []
                    for tb in range(4):
                        s = ri % 2
                        ri += 1
                        A, B = ra[s], rb[s]
                        pv = ps[tb][:].rearrange("p (h x) -> p h x", x=128)
                        Bv = B[:].rearrange("p (h x) -> p h x", x=128)
                        sv = sinb[:, tb, :].rearrange("p (h x) -> p h x", x=128)
                        o1 = ph.op("dve", lambda e, A=A, tb=tb: e.tensor_tensor(out=A[:], in0=ps[tb][:], in1=cosb[:, tb, :], op=ALU.mult),
                                   [fin[tb], cst1, cst2, ra_free[s]])
                        o2 = ph.op("dve", lambda e, pv=pv, Bv=Bv, sv=sv: e.tensor_tensor(
                            out=Bv[:, :, 0:64], in0=pv[:, :, 64:128], in1=sv[:, :, 0:64], op=ALU.mult), [o1])
                        o3 = ph.op("dve", lambda e, pv=pv, Bv=Bv, sv=sv: e.tensor_tensor(
                            out=Bv[:, :, 64:128], in0=pv[:, :, 0:64], in1=sv[:, :, 64:128], op=ALU.mult), [o2])
                        ps_free[tb] = o3
                        o4 = ph.op("dve", lambda e, A=A, B=B, tb=tb: e.tensor_tensor(out=rq[tb][:], in0=A[:], in1=B[:], op=ALU.add),
                                   [o3, rq_free[tb]])
                        ra_free[s] = o4
                        rope_toks.append(o4)
                        last_rope = o4
                    dstT = self.qT if kind == 0 else self.kT
                    head0 = (ch % (D // 512)) * 4
                    for hp in range(2):
                        bi = qi % 2
                        qi += 1
                        bank = tq[bi]
                        tt = None
                        for hh in range(2):
                            hd = hp * 2 + hh
                            for tb in range(4):
                                w = [rope_toks[tb]]
                                if hh == 0 and tb == 0:
                                    w.append(tq_free[bi])
                                lastt = (hh == 1 and tb == 3)
                                tt = ph.op("pe", lambda e, bank=bank, hh=hh, tb=tb, hd=hd: e.transpose(
                                    out=bank[:, hh * 512 + tb * 128: hh * 512 + (tb + 1) * 128],
                                    in_=rq[tb][:, hd * 128:(hd + 1) * 128], identity=self.ident_bf[:]),
                                    w, signal=lastt or (hp == 1 and hh == 1))
                                if hp == 1 and hh == 1:
                                    rq_free[tb] = tt
                        ev = ph.op("act", lambda e, bank=bank, bi=bi: e.activation(
                            out=qst[bi][:].rearrange("p a t -> p (a t)"), in_=bank[:], func=AF.Copy),
                            [tt, qst_free[bi]])
                        tq_free[bi] = ev
                        rows = slice((head0 + hp * 2) * 128, (head0 + hp * 2 + 2) * 128)
                        st = ph.dma("act", dstT[rows, r0:r0 + 512].rearrange("(a p) t -> p a t", p=128), qst[bi][:], qst_ds[bi], [ev])
                        qst_free[bi] = st
                        finals.append(st)
            hT_free = last_mm
            if ti + 1 < c.NT:
                hT_tok = self.norm_b(ph, nb, na_toks, hT, hT_free)
            cs_free = last_rope
        ph.finish({"act": finals[-40:], "sp": [], "dve": []})

    def phase_att(self, layer):
        c = self.cfg
        D, SL, NT = c.D, c.SL, c.NT
        j = layer // 2
        lam_init = 0.8 - 0.6 * math.exp(-0.3 * layer)
        scale = 128 ** -0.5
        ph = Phase(self, f"att{layer}")
        self.issue_conversions(ph)
        NKB = SL // 128
        KT = [[ph.sbuf(f"KT{b}_{cc}", [128, SL], BF16) for cc in range(2)] for b in range(2)]
        KT_ds = [[ph.dsem(f"KTd{b}_{cc}") for cc in range(2)] for b in range(2)]
        VA = [ph.sbuf(f"VA{b}", [128, NKB, 258], BF16) for b in range(2)]
        VA_ds = [ph.dsem(f"VAd{b}") for b in range(2)]
        QT = [[ph.sbuf(f"QT{b}_{cc}", [128, 512], BF16) for cc in range(2)] for b in range(2)]
        QT_ds = [[ph.dsem(f"QTd{b}_{cc}") for cc in range(2)] for b in range(2)]
        PT = [ph.sbuf(f"PT{i}", [128, 512], BF16) for i in range(3)]
        psS = [ph.psum(f"psS{i}", [128, 512], F32) for i in range(2)]
        psO = [ph.psum(f"psO{i}", [128, 512], F32) for i in range(4)]
        t1 = [ph.sbuf(f"t1_{i}", [128, 256], F32) for i in range(4)]
        ocb = [ph.sbuf(f"ocb{i}", [128, 256], F32) for i in range(2)]
        obf = [ph.sbuf(f"obf{i}", [128, 256], BF16) for i in range(4)]
        obf_ds = [ph.dsem(f"obfd{i}") for i in range(4)]
        sm = {n: [ph.sbuf(f"{n}{i}", [128, 1], F32) for i in range(2)] for n in ("r1", "r2", "ss", "lnv", "rs")}
        junk = ph.sbuf("junk", [128, 256], BF16)
        lt = [ph.sbuf(f"lt{i}", [128, 128], F32) for i in range(4)]
        lds = [ph.dsem(f"ltd{i}") for i in range(4)]
        lj = ph.sbuf("lj", [128, 128], F32)
        ls = [ph.sbuf(f"ls{i}", [128, 1], F32) for i in range(2)]
        le = [ph.sbuf(f"le{i}", [128, 1], F32) for i in range(2)]
        neglam = ph.sbuf("neglam", [128, 1], F32)
        subg = ph.sbuf("subg", [128, 256], F32)
        sg_ds = ph.dsem("sgd")
        ltok = [ph.dma("sp", lt[i][:], self.lam_bc[j, i], lds[i]) for i in range(4)]
        sgtok = ph.dma("sp", subg[:], self.subg_bc[j], sg_ds)
        a1 = ph.op("dve", lambda e: e.scalar_tensor_tensor(out=lj[:], in0=lt[0][:], scalar=1.0, in1=lt[1][:],
                                                            op0=ALU.mult, op1=ALU.mult, accum_out=ls[0][:]), [ltok[0], ltok[1]])
        a2 = ph.op("dve", lambda e: e.scalar_tensor_tensor(out=lj[:], in0=lt[2][:], scalar=1.0, in1=lt[3][:],
                                                            op0=ALU.mult, op1=ALU.mult, accum_out=ls[1][:]), [ltok[2], ltok[3], a1])
        e1 = ph.op("act", lambda e: e.activation(out=le[0][:], in_=ls[0][:], func=AF.Exp), [a1])
        e2 = ph.op("act", lambda e: e.activation(out=le[1][:], in_=ls[1][:], func=AF.Exp), [a2])
        a3 = ph.op("dve", lambda e: e.tensor_tensor(out=neglam[:], in0=le[1][:], in1=le[0][:], op=ALU.subtract), [e1, e2])
        a4 = ph.op("dve", lambda e: e.tensor_scalar(out=neglam[:], in0=neglam[:], scalar1=-lam_init, scalar2=None, op0=ALU.add), [a3])
        a5 = ph.op("dve", lambda e: e.tensor_scalar(out=subg[:], in0=subg[:], scalar1=(1.0 - lam_init), scalar2=None, op0=ALU.mult), [sgtok])
        ones_tok = [ph.op("dve", lambda e, b=b: e.memset(VA[b][:, :, 256:257], 1.0)) for b in range(2)]
        const_tok = [a4, a5]

        KT_free = [[None, None], [None, None]]
        VA_free = [None, None]
        QT_free = [[None, None], [None, None]]
        PT_free = [None] * 3
        psS_free = [None, None]
        psO_free = [None] * 4
        t1_free = [None] * 4
        ocb_free = [None, None]
        obf_free = [None] * 4
        sm_free = [None, None]
        junk_tok = None
        finals = []
        si = 0
        qi = 0
        oi = 0
        ei = 0
        for h in range(c.NH):
            hb = h % 2
            ktok = []
            for cc in range(2):
                comp = 2 * h + cc
                ktok.append(ph.dma("sp", KT[hb][cc][:], self.kT[comp * 128:(comp + 1) * 128, :], KT_ds[hb][cc], [KT_free[hb][cc]]))
            vtok = ph.dma("sp", VA[hb][:, :, 0:256], self.v[:, h * 256:(h + 1) * 256].rearrange("(b p) e -> p b e", p=128),
                          VA_ds[hb], [VA_free[hb], ones_tok[hb]])
            last_pv_head = None
            last_s_head = [None, None]
            for Q in range(c.lo[layer], NT):
                qb_ = qi % 2
                qi += 1
                qtok = []
                for cc in range(2):
                    comp = 2 * h + cc
                    qtok.append(ph.dma("sp", QT[qb_][cc][:], self.qT[comp * 128:(comp + 1) * 128, Q * 512:(Q + 1) * 512],
                                       QT_ds[qb_][cc], [QT_free[qb_][cc]]))
                for cc in range(2):
                    nkb = 4 * Q + 4
                    s_toks = {}
                    p_toks = {}

                    def emit_S(kb):
                        nonlocal si
                        sb_ = si % 2
                        si += 1
                        w = [psS_free[sb_]]
                        if kb == 0:
                            w += [ktok[cc], qtok[cc]]
                        tk = ph.op("pe", lambda e, sb_=sb_, kb=kb, kt=KT[hb][cc], qt=QT[qb_][cc]: e.matmul(
                            psS[sb_][:], lhsT=kt[:, kb * 128:(kb + 1) * 128], rhs=qt[:], start=True, stop=True), w)
                        s_toks[kb] = (tk, sb_)
                        return tk

                    def emit_exp(kb):
                        nonlocal ei
                        tk, sb_ = s_toks[kb]
                        pb = ei % 3
                        ei += 1
                        use_ctx = (Q >= c.CTXT and kb < c.CTXT * 4)
                        if use_ctx:
                            f = lambda e, sb_=sb_, pb=pb: e.activation(out=PT[pb][:], in_=psS[sb_][:], func=AF.Exp,
                                                                        bias=self.ctxb_sb[:], scale=scale)
                        else:
                            f = lambda e, sb_=sb_, pb=pb: e.activation(out=PT[pb][:], in_=psS[sb_][:], func=AF.Exp, scale=scale)
                        ex = ph.op("act", f, [tk, PT_free[pb]])
                        psS_free[sb_] = ex
                        v = kb - 4 * Q
                        if v >= 0:
                            ex = ph.op("dve", lambda e, pb=pb, v=v: e.tensor_tensor(
                                out=PT[pb][:], in0=PT[pb][:], in1=self.dmask_bf[:, v, :], op=ALU.mult), [ex])
                        p_toks[kb] = (ex, pb)

                    def emit_PV(kb):
                        ex, pb = p_toks[kb]
                        v = kb - 4 * Q
                        lastt = None
                        for jq in range(4):
                            if v >= 0 and jq < v:
                                continue
                            w = []
                            if lastt is None:
                                w += [ex]
                                if kb == 0:
                                    w += [vtok]
                            if kb == 0:
                                w += [psO_free[jq]]
                            stop = (kb == 4 * Q + jq)
                            lastt = ph.op("pe", lambda e, jq=jq, pb=pb, kb=kb, stop=stop, va=VA[hb]: e.matmul(
                                psO[jq][:, 0:257], lhsT=PT[pb][:, jq * 128:(jq + 1) * 128], rhs=va[:, kb, 0:257],
                                start=(kb == 0), stop=stop), w, signal=(stop or jq == 3))
                            if stop:
                                o_fin[jq] = lastt
                        PT_free[pb] = lastt
                        return lastt

                    o_fin = [None] * 4
                    emit_S(0)
                    lastpv = None
                    for kb in range(nkb):
                        if kb + 1 < nkb:
                            emit_S(kb + 1)
                        emit_exp(kb)
                        lastpv = emit_PV(kb)
                    last_pv_head = lastpv
                    last_s_head[cc] = s_toks[nkb - 1][0]
                    for jq in range(4):
                        m = oi % 2
                        if cc == 0:
                            r = ph.op("dve", lambda e, jq=jq, m=m: e.reciprocal(out=sm["r1"][m][:], in_=psO[jq][:, 256:257]),
                                      [o_fin[jq], sm_free[m]])
                            tt = ph.op("dve", lambda e, jq=jq, m=m: e.tensor_scalar(
                                out=t1[jq][:], in0=psO[jq][:, 0:256], scalar1=sm["r1"][m][:], scalar2=None, op0=ALU.mult),
                                [r, t1_free[jq]])
                            psO_free[jq] = tt
                            sm_free[m] = tt
                            oi += 1
                        else:
                            r = ph.op("dve", lambda e, jq=jq, m=m: e.reciprocal(out=sm["r2"][m][:], in_=psO[jq][:, 256:257]),
                                      [o_fin[jq], sm_free[m]] + const_tok)
                            r2 = ph.op("dve", lambda e, m=m: e.tensor_tensor(out=sm["r2"][m][:], in0=sm["r2"][m][:], in1=neglam[:], op=ALU.mult), [r])
                            oc = ph.op("dve", lambda e, jq=jq, m=m: e.scalar_tensor_tensor(
                                out=ocb[m][:], in0=psO[jq][:, 0:256], scalar=sm["r2"][m][:], in1=t1[jq][:], op0=ALU.mult, op1=ALU.add),
                                [r2, ocb_free[m]])
                            psO_free[jq] = oc
                            t1_free[jq] = oc
                            sq = ph.op("act", lambda e, m=m: e.activation(out=junk[:], in_=ocb[m][:], func=AF.Square, accum_out=sm["ss"][m][:]),
                                       [oc, junk_tok])
                            junk_tok = sq
                            ln = ph.op("act", lambda e, m=m: e.activation(out=sm["lnv"][m][:], in_=sm["ss"][m][:], func=AF.Ln,
                                                                           bias=self.eps_sub[:], scale=1.0 / 256), [sq])
                            rs = ph.op("act", lambda e, m=m: e.activation(out=sm["rs"][m][:], in_=sm["lnv"][m][:], func=AF.Exp, scale=-0.5), [ln])
                            ob = oi % 4
                            fo = ph.op("dve", lambda e, m=m, ob=ob: e.scalar_tensor_tensor(
                                out=obf[ob][:], in0=ocb[m][:], scalar=sm["rs"][m][:], in1=subg[:], op0=ALU.mult, op1=ALU.mult),
                                [rs, obf_free[ob]])
                            ocb_free[m] = fo
                            sm_free[m] = fo
                            rows = slice(Q * 512 + jq * 128, Q * 512 + (jq + 1) * 128)
                            st = ph.dma("act", self.osc[rows, h * 256:(h + 1) * 256], obf[ob][:], obf_ds[ob], [fo])
                            obf_free[ob] = st
                            finals.append(st)
                            oi += 1
                    QT_free[qb_][cc] = s_toks[nkb - 1][0]
            for cc in range(2):
                KT_free[hb][cc] = last_s_head[cc]
            VA_free[hb] = last_pv_head
        ph.finish({"act": finals[-8:]})

    def phase_oproj(self, layer):
        c = self.cfg
        D, KD = c.D, c.KD
        j = layer // 2
        ph = Phase(self, f"opj{layer}")
        self.issue_conversions(ph)
        ring = Ring(ph, c.ring)
        epi = self.epi_bufs(ph)
        oT = ph.sbuf("oT", [128, KD, 512], BF16)
        ob = [ph.sbuf(f"ob{i}", [128, D], BF16) for i in range(2)]
        ob_ds = [ph.dsem(f"obd{i}") for i in range(2)]
        ob_free = [None, None]
        tp = [ph.psum(f"tp{i}", [128, 1024], BF16) for i in range(2)]
        tp_free = [None, None]
        ps = [ph.psum(f"ps{i}", [128, 512], F32) for i in range(4)]
        ps_free = [None] * 4
        oT_free = None
        res_src = self.x if layer == 0 else self.hres
        allst = []
        tpi = 0
        for ti in range(c.lo[layer], c.NT):
            r0 = ti * 512
            last_ev = None
            for b in range(4):
                s = b % 2
                ld = ph.dma("sp", ob[s][:], self.osc[r0 + b * 128:r0 + (b + 1) * 128, :], ob_ds[s], [ob_free[s]])
                tt = None
                for half in range((KD + 7) // 8):
                    k0 = half * 8
                    nk = min(8, KD - k0)
                    bi = tpi % 2
                    tpi += 1
                    for kk in range(nk):
                        k = k0 + kk
                        w = [ld, tp_free[bi]] if kk == 0 else []
                        tt = ph.op("pe", lambda e, s=s, bi=bi, k=k, kk=kk: e.transpose(
                            out=tp[bi][:, kk * 128:(kk + 1) * 128], in_=ob[s][:, k * 128:(k + 1) * 128], identity=self.ident_bf[:]),
                            w, signal=(kk == nk - 1))
                    ev = ph.op("act", lambda e, bi=bi, k0=k0, nk=nk, b=b: e.activation(
                        out=oT[:, k0:k0 + nk, b * 128:(b + 1) * 128],
                        in_=tp[bi][:, 0:nk * 128].rearrange("p (k t) -> p k t", t=128), func=AF.Copy), [tt, oT_free])
                    tp_free[bi] = ev
                    last_ev = ev
                ob_free[s] = tt
            last_mm, sts = self.proj_tm_residual(ph, ring, self.wb["w_o_attn"][j], self.cv_tok[("w_o_attn", j)], KD,
                                                 oT, last_ev, ps, ps_free, epi, res_src, self.hres, r0)
            oT_free = last_mm
            allst += sts
        ph.finish({"act": allst[-8:]})

    def phase_ffn(self, layer):
        c = self.cfg
        D, KD, KF = c.D, c.KD, c.KF
        ph = Phase(self, f"ffn{layer}")
        self.issue_conversions(ph)
        ring = Ring(ph, c.ring)
        nb = self.norm_bufs(ph)
        epi = self.epi_bufs(ph)
        gbc, gtok = self.load_gain(ph, c.n_attn + c.n_conv + layer)
        hT = ph.sbuf("hT", [128, KD, 512], BF16)
        actT = ph.sbuf("actT", [128, KF, 512], BF16)
        pg = [ph.psum(f"pg{i}", [128, 512], F32) for i in range(3)]
        pu = [ph.psum(f"pu{i}", [128, 512], F32) for i in range(3)]
        pp_free = [None] * 3
        sg = [ph.sbuf(f"sg{i}", [128, 512], F32) for i in range(2)]
        sg_free = [None, None]
        ps_free = [None] * 4
        hT_free = None
        actT_free = None
        wg, wu, wd = self.wb["w_gate"][layer], self.wb["w_up"][layer], self.wb["w_down"][layer]
        tg, tu, td = self.cv_tok[("w_gate", layer)], self.cv_tok[("w_up", layer)], self.cv_tok[("w_down", layer)]
        allst = []
        pi = 0
        tiles = list(range(c.lo[layer], c.NT))
        na_toks = self.norm_a(ph, nb, self.hres[tiles[0] * 512:tiles[0] * 512 + 512, :], gbc, gtok)
        hT_tok = self.norm_b(ph, nb, na_toks, hT, None)
        for tix, ti in enumerate(tiles):
            r0 = ti * 512
            nxt = tiles[tix + 1] if tix + 1 < len(tiles) else None
            if nxt is not None:
                na_toks = self.norm_a(ph, nb, self.hres[nxt * 512:nxt * 512 + 512, :], gbc, gtok)
            last_mm = None
            last_act = None
            for i in range(c.DFF // 512):
                gt, gl, gi = ring.load(wg[:, i * 512:(i + 1) * 512], KD, [tg])
                ut, ul, ui = ring.load(wu[:, i * 512:(i + 1) * 512], KD, [tu])
                for m in range(4):
                    p = pi % 3
                    pi += 1
                    mg = None
                    for k in range(KD):
                        w = []
                        if k == 0:
                            w += [pp_free[p]]
                            if m == 0:
                                w += [gl, hT_tok]
                        mg = ph.op("pe", lambda e, p=p, gt=gt, m=m, k=k: e.matmul(
                            pg[p][:], lhsT=gt[:, k, m * 128:(m + 1) * 128], rhs=hT[:, k, :], start=(k == 0), stop=(k == KD - 1)),
                            w, signal=(k == KD - 1))
                    mu = None
                    for k in range(KD):
                        w = [ul] if (k == 0 and m == 0) else []
                        mu = ph.op("pe", lambda e, p=p, ut=ut, m=m, k=k: e.matmul(
                            pu[p][:], lhsT=ut[:, k, m * 128:(m + 1) * 128], rhs=hT[:, k, :], start=(k == 0), stop=(k == KD - 1)),
                            w, signal=(k == KD - 1))
                    s = pi % 2
                    sl = ph.op("act", lambda e, p=p, s=s: e.activation(out=sg[s][:], in_=pg[p][:], func=AF.Silu), [mg, sg_free[s]])
                    ml = ph.op("dve", lambda e, p=p, s=s, i=i, m=m: e.tensor_tensor(
                        out=actT[:, i * 4 + m, :], in0=sg[s][:], in1=pu[p][:], op=ALU.mult), [sl, mu, actT_free])
                    sg_free[s] = ml
                    pp_free[p] = ml
                    last_act = ml
                ring.release(gi, mu)
                ring.release(ui, mu)
                last_mm = mu
            hT_free = last_mm
            cur_hT_tok = hT_tok
            if nxt is not None:
                hT_tok = self.norm_b(ph, nb, na_toks, hT, hT_free)
            ps = [pg[0], pg[1], pu[0], pu[1]]
            ps_free = [last_act] * 4
            lmm, sts = self.proj_tm_residual(ph, ring, wd, td, KF, actT, last_act, ps, ps_free, epi, self.hres, self.hres, r0)
            actT_free = lmm
            pp_free = [ps_free[3]] * 3
            allst += sts
        ph.finish({"act": allst[-8:]})

    def phase_conv(self, layer):
        c = self.cfg
        D, KD = c.D, c.KD
        j = layer // 2
        ph = Phase(self, f"cnv{layer}")
        self.issue_conversions(ph)
        ring = Ring(ph, c.ring)
        nb = self.norm_bufs(ph)
        epi = self.epi_bufs(ph)
        gbc, gtok = self.load_gain(ph, c.n_attn + j)
        hT = ph.sbuf("hT", [128, KD, 512], BF16)
        yT = ph.sbuf("yT", [128, KD, 512], BF16)
        cw = ph.sbuf("cw", [128, KD, 3], F32)
        cw_ds = ph.dsem("cwd")
        cwtok = ph.dma("sp", cw[:], self.convw_t[j], cw_ds)
        halo = ph.sbuf("halo", [128, KD, 2], F32)
        hz = ph.op("dve", lambda e: e.memset(halo[:], 0.0))
        pb_ = [ph.psum(f"pb{i}", [128, 512], F32) for i in range(2)]
        pc_ = [ph.psum(f"pc{i}", [128, 512], F32) for i in range(2)]
        pu_ = [ph.psum(f"pu{i}", [128, 512], F32) for i in range(2)]
        pp_free = [None, None]
        gcs = [ph.sbuf(f"gcs{i}", [128, 512], F32) for i in range(2)]
        zb = [ph.sbuf(f"zb{i}", [128, 514], F32) for i in range(2)]
        zc = [ph.sbuf(f"zc{i}", [128, 512], F32) for i in range(2)]
        buf_free = [None, None]
        wsrc = self.wb["w_bch"][j]
        wtok = self.cv_tok[("w_bch", j)]
        ps_free = [None] * 4
        hT_free = None
        yT_free = None
        halo_tok = {m: hz for m in range(KD)}
        allst = []
        pi = 0
        nD = D // 512
        tiles = list(range(c.lo[layer], c.NT))
        na_toks = self.norm_a(ph, nb, self.hres[tiles[0] * 512:tiles[0] * 512 + 512, :], gbc, gtok)
        hT_tok = self.norm_b(ph, nb, na_toks, hT, None)
        for tix, ti in enumerate(tiles):
            r0 = ti * 512
            nxt = tiles[tix + 1] if tix + 1 < len(tiles) else None
            if nxt is not None:
                na_toks = self.norm_a(ph, nb, self.hres[nxt * 512:nxt * 512 + 512, :], gbc, gtok)
            if ti == c.CTXT:
                hf = ph.op("dve", lambda e: e.tensor_scalar(out=halo[:], in0=halo[:], scalar1=self.hflag_sb[:], scalar2=None, op0=ALU.mult),
                           list(halo_tok.values()))
                halo_tok = {m: hf for m in range(KD)}
            last_mm = None
            last_y = None
            for i in range(nD):
                bt, bl, bi_ = ring.load(wsrc[:, i * 512:(i + 1) * 512], KD, [wtok])
                ct, cl, ci_ = ring.load(wsrc[:, D + i * 512:D + (i + 1) * 512], KD, [wtok])
                ut, ul, ui_ = ring.load(wsrc[:, 2 * D + i * 512:2 * D + (i + 1) * 512], KD, [wtok])
                for m in range(4):
                    p = pi % 2
                    pi += 1
                    mm = {}
                    for nm, wt, lt_, pst in (("b", bt, bl, pb_), ("c", ct, cl, pc_), ("u", ut, ul, pu_)):
                        for k in range(KD):
                            w = []
                            if k == 0 and nm == "b":
                                w += [pp_free[p]]
                            if k == 0 and m == 0:
                                w += [lt_, hT_tok]
                            mm[nm] = ph.op("pe", lambda e, pst=pst, p=p, wt=wt, m=m, k=k: e.matmul(
                                pst[p][:], lhsT=wt[:, k, m * 128:(m + 1) * 128], rhs=hT[:, k, :], start=(k == 0), stop=(k == KD - 1)),
                                w, signal=(k == KD - 1))
                    fm = i * 4 + m
                    g1 = ph.op("act", lambda e, p=p: e.activation(out=gcs[p][:], in_=pc_[p][:], func=AF.Copy), [mm["c"], buf_free[p]])
                    z1 = ph.op("dve", lambda e, p=p: e.tensor_tensor(out=zb[p][:, 2:514], in0=gcs[p][:], in1=pu_[p][:], op=ALU.mult),
                               [g1, mm["u"]])
                    z0 = ph.op("dve", lambda e, p=p, fm=fm: e.tensor_copy(out=zb[p][:, 0:2], in_=halo[:, fm, :]), [halo_tok[fm], z1, cwtok])
                    c1 = ph.op("dve", lambda e, p=p, fm=fm: e.tensor_scalar(
                        out=zc[p][:], in0=zb[p][:, 0:512], scalar1=cw[:, fm, 0:1], scalar2=None, op0=ALU.mult), [z0])
                    c2 = ph.op("dve", lambda e, p=p, fm=fm: e.scalar_tensor_tensor(
                        out=zc[p][:], in0=zb[p][:, 1:513], scalar=cw[:, fm, 1:2], in1=zc[p][:], op0=ALU.mult, op1=ALU.add), [c1])
                    c3 = ph.op("dve", lambda e, p=p, fm=fm: e.scalar_tensor_tensor(
                        out=zc[p][:], in0=zb[p][:, 2:514], scalar=cw[:, fm, 2:3], in1=zc[p][:], op0=ALU.mult, op1=ALU.add), [c2])
                    hn = ph.op("dve", lambda e, p=p, fm=fm: e.tensor_copy(out=halo[:, fm, :], in_=zb[p][:, 512:514]), [c3])
                    halo_tok[fm] = hn
                    y1 = ph.op("dve", lambda e, p=p, fm=fm: e.tensor_tensor(out=yT[:, fm, :], in0=zc[p][:], in1=pb_[p][:], op=ALU.mult),
                               [hn, mm["b"], yT_free])
                    pp_free[p] = y1
                    buf_free[p] = y1
                    last_y = y1
                for idx in (bi_, ci_, ui_):
                    ring.release(idx, mm["u"])
                last_mm = mm["u"]
            hT_free = last_mm
            if nxt is not None:
                hT_tok = self.norm_b(ph, nb, na_toks, hT, hT_free)
            ps = [pb_[0], pb_[1], pc_[0], pc_[1]]
            ps_free = [last_y] * 4
            lmm, sts = self.proj_tm_residual(ph, ring, self.wb["w_o_conv"][j], self.cv_tok[("w_o_conv", j)], KD,
                                             yT, last_y, ps, ps_free, epi, self.hres, self.hres, r0)
            yT_free = lmm
            pp_free = [ps_free[3], ps_free[3]]
            allst += sts
        ph.finish({"act": allst[-8:]})

    def phase_final(self):
        c = self.cfg
        D = c.D
        ph = Phase(self, "final")
        gbc, gtok = self.load_gain(ph, c.n_attn + c.n_conv + c.depth)
        xs = [ph.sbuf(f"xs{i}", [128, D], F32) for i in range(2)]
        xs_ds = [ph.dsem(f"xsd{i}") for i in range(2)]
        yo = [ph.sbuf(f"yo{i}", [128, D], F32) for i in range(2)]
        yo_ds = [ph.dsem(f"yod{i}") for i in range(2)]
        ss = [ph.sbuf(f"ss{i}", [128, 1], F32) for i in range(2)]
        lnv = [ph.sbuf(f"lnv{i}", [128, 1], F32) for i in range(2)]
        rs = [ph.sbuf(f"rs{i}", [128, 1], F32) for i in range(2)]
        junk = ph.sbuf("junk", [128, D], BF16)
        xs_free = [None, None]
        yo_free = [None, None]
        jt = None
        sts = []
        nblk = (c.NT - c.out_lo) * 4
        for b in range(nblk):
            s = b % 2
            r0 = c.out_lo * 512 + b * 128
            ld = ph.dma("sp", xs[s][:], self.hres[r0:r0 + 128, :], xs_ds[s], [xs_free[s]])
            sq = ph.op("act", lambda e, s=s: e.activation(out=junk[:], in_=xs[s][:], func=AF.Square, accum_out=ss[s][:]), [ld, jt])
            jt = sq
            ln = ph.op("act", lambda e, s=s: e.activation(out=lnv[s][:], in_=ss[s][:], func=AF.Ln, bias=self.eps_rms[:], scale=1.0 / D), [sq])
            rr = ph.op("act", lambda e, s=s: e.activation(out=rs[s][:], in_=lnv[s][:], func=AF.Exp, scale=-0.5), [ln])
            yy = ph.op("dve", lambda e, s=s: e.scalar_tensor_tensor(out=yo[s][:], in0=xs[s][:], scalar=rs[s][:], in1=gbc[:],
                                                                     op0=ALU.mult, op1=ALU.mult), [rr, gtok, yo_free[s]])
            xs_free[s] = yy
            st = ph.dma("act", self.out[b * 128:(b + 1) * 128, :], yo[s][:], yo_ds[s], [yy])
            yo_free[s] = st
            sts.append(st)
        ph.finish({"act": sts[-2:]})

    def build(self):
        c = self.cfg
        self.phase_setup()
        for i in range(c.depth):
            if i % 2 == 0:
                self.phase_qkv(i)
                self.phase_att(i)
                self.phase_oproj(i)
            else:
                self.phase_conv(i)
            self.phase_ffn(i)
        self.phase_final()
        self.top.close()
        return self.nc


def host_tables(cfg, half):
    SL, CT = cfg.SL, cfg.CTXT * 512
    pos = np.arange(SL, dtype=np.float32)
    if half == 0:
        pos = np.maximum(pos - CT, 0.0).astype(np.float32)
    inv_freq = (1.0 / (10000.0 ** (np.arange(0, 128, 2, dtype=np.float32) / 128.0))).astype(np.float32)
    ang = pos[:, None] * inv_freq[None, :]
    cos, sin = np.cos(ang).astype(np.float32), np.sin(ang).astype(np.float32)
    cosF = np.concatenate([cos, cos], axis=1)
    sinF = np.concatenate([-sin, sin], axis=1)
    rope_cos = np.ascontiguousarray(np.tile(cosF, (1, 4)))
    rope_sin = np.ascontiguousarray(np.tile(sinF, (1, 4)))
    kp = np.arange(128)[:, None, None]
    v = np.arange(4)[None, :, None]
    qf = np.arange(512)[None, None, :]
    dmask = (v * 128 + kp <= qf).astype(np.float32)
    ctxb = np.full((128, 1), 0.0 if half == 1 else NEG_BIG, np.float32)
    hflag = np.full((128, 1), 1.0 if half == 1 else 0.0, np.float32)
    return dict(rope_cos=rope_cos, rope_sin=rope_sin, dmask=dmask, ctxb=ctxb, hflag=hflag,
                ident=np.eye(128, dtype=np.float32))


def host_params(cfg, inp):
    f = np.float32
    def bc(a):
        a = np.asarray(a, f)
        return np.ascontiguousarray(np.broadcast_to(a[..., None, :], a.shape[:-1] + (128, a.shape[-1])))
    gains = np.concatenate([np.asarray(inp["attn_norm_g"], f), np.asarray(inp["conv_norm_g"], f),
                            np.asarray(inp["ffn_norm_g"], f), np.asarray(inp["final_norm_g"], f)[None]], axis=0)
    lam = np.stack([np.asarray(inp[k], f) for k in ("lambda_q1", "lambda_k1", "lambda_q2", "lambda_k2")], axis=1)
    convw = np.asarray(inp["conv_w"], f)
    ncv = convw.shape[0]
    convw_t = np.ascontiguousarray(convw.reshape(ncv, 3, cfg.KD, 128).transpose(0, 3, 2, 1))
    d = dict(gains_bc=bc(gains), lam_bc=bc(lam), subg_bc=bc(np.asarray(inp["subln_g"], f)), convw_t=convw_t)
    for k in ("w_qkv", "w_o_attn", "w_bch", "w_o_conv", "w_gate", "w_up", "w_down"):
        d[k] = np.ascontiguousarray(np.asarray(inp[k], f))
    return d


_NC_CACHE = {}


def run_model(cfg, inp, n_cores=None):
    key = (cfg.D, cfg.DFF, cfg.SL, cfg.depth, cfg.mode)
    if key not in _NC_CACHE:
        _NC_CACHE[key] = Prog(cfg).build()
    nc = _NC_CACHE[key]
    x = np.asarray(inp["x"], np.float32)
    B, S, D = x.shape
    CT = cfg.CTXT * 512
    params = host_params(cfg, inp)
    tabs = [host_tables(cfg, 0), host_tables(cfg, 1)]
    in_maps = []
    if cfg.mode == "A":
        n_cores = B if n_cores is None else n_cores
        for c in range(n_cores):
            m = dict(params)
            m.update(tabs[1])
            m["x"] = np.ascontiguousarray(x[c % B])
            in_maps.append(m)
        res = run_bass_kernel_spmd(nc, in_maps, core_ids=list(range(n_cores)))
        return np.stack([res.results[b]["out"] for b in range(B)], axis=0).astype(np.float32)
    n_cores = 2 * B if n_cores is None else n_cores
    for c in range(n_cores):
        b, half = (c // 2) % B, c % 2
        if half == 1:
            xl = x[b]
        else:
            xl = np.concatenate([np.zeros((CT, D), np.float32), x[b, :S - CT]], axis=0)
        m = dict(params)
        m.update(tabs[half])
        m["x"] = np.ascontiguousarray(xl)
        in_maps.append(m)
    res = run_bass_kernel_spmd(nc, in_maps, core_ids=list(range(n_cores)))
    out = np.zeros((B, S, D), np.float32)
    for c in range(min(n_cores, 2 * B)):
        b, half = c // 2, c % 2
        o = res.results[c]["out"]
        if half == 0:
            out[b, :S - CT] = o
        else:
            out[b, CT:] = o
    return out


def kernel(**inputs):
    cfg = Cfg()
    return run_model(cfg, inputs)
```

```python
import math
from contextlib import ExitStack

import numpy as np
import concourse.bass as bass
import concourse.mybir as mybir
from concourse.bass_utils import run_bass_kernel_spmd

F32 = mybir.dt.float32
BF16 = mybir.dt.bfloat16
AF = mybir.ActivationFunctionType
ALU = mybir.AluOpType

RMS_EPS = 1e-6
SUBLN_EPS = 1e-5
NEG_BIG = -30000.0


class Cfg:
    def __init__(self, D=2048, DFF=5632, SL=4096, depth=4, ring=5, mode="A"):
        self.mode = mode
        self.D, self.DFF, self.SL, self.depth = D, DFF, SL, depth
        self.T = 512
        self.NT = SL // 512
        self.KD = D // 128
        self.KF = DFF // 128
        self.NH = D // 256
        self.NCOMP = 2 * self.NH
        self.CTXT = self.NT // 2
        self.ring = ring
        self.n_attn = (depth + 1) // 2
        self.n_conv = depth // 2
        if mode == "B":
            self.lo = [0 if i < 2 else self.CTXT - 1 for i in range(depth)]
            self.out_lo = self.CTXT
        else:
            self.lo = [0] * depth
            self.out_lo = 0


class DSem:
    def __init__(self, h):
        self.h, self.n = h, 0


class Phase:
    ENG = ("pe", "act", "dve", "pool", "sp")

    def __init__(self, P, name):
        self.P, self.nc, self.name = P, P.nc, name
        self.ops = {e: [] for e in self.ENG}
        self.stack = ExitStack()
        self.prog = {}
        self.seq = {}
        for e in ("pe", "act", "dve", "pool"):
            self.prog[e] = self.stack.enter_context(self.nc.semaphore(f"{name}_{e}"))
            self.seq[e] = 0
        self.dsems = []

    def dsem(self, name):
        h = self.stack.enter_context(self.nc.semaphore(f"{self.name}_{name}"))
        d = DSem(h)
        self.dsems.append(d)
        return d

    def sbuf(self, name, shape, dt):
        return self.stack.enter_context(self.nc.sbuf_tensor(f"{self.name}_{name}", shape, dt))

    def psum(self, name, shape, dt):
        return self.stack.enter_context(self.nc.psum_tensor(f"{self.name}_{name}", shape, dt))

    @staticmethod
    def _emit_waits(e, waits):
        best = {}
        for w in waits:
            if w is None:
                continue
            h, v = w
            k = id(h)
            if k not in best or best[k][1] < v:
                best[k] = (h, v)
        for h, v in best.values():
            e.wait_ge(h, v)

    def op(self, eng, f, waits=(), signal=True):
        waits = [w for w in waits if w is not None]
        tok = None
        if signal:
            self.seq[eng] += 1
            tok = (self.prog[eng], self.seq[eng])
            sem = self.prog[eng]

            def run(e, f=f, waits=waits, sem=sem):
                self._emit_waits(e, waits)
                f(e).then_inc(sem, 1)
        else:
            def run(e, f=f, waits=waits):
                self._emit_waits(e, waits)
                f(e)
        self.ops[eng].append(run)
        return tok

    def dma(self, queue, out, in_, dsem, waits=()):
        waits = [w for w in waits if w is not None]
        dsem.n += 16
        tok = (dsem.h, dsem.n)

        def run(e, out=out, in_=in_, waits=waits, h=dsem.h):
            self._emit_waits(e, waits)
            e.dma_start(out=out, in_=in_).then_inc(h, 16)
        self.ops[queue].append(run)
        return tok

    def finish(self, final_waits=()):
        fw = {e: [] for e in self.ENG}
        for e, toks in dict(final_waits).items():
            fw[e] = list(toks)
        ops = self.ops
        with self.nc.Block(self.name, no_gpsimd_drain=True) as block:
            @block.tensor
            def _(e):
                for f in ops["pe"]:
                    f(e)
                self._emit_waits(e, fw["pe"])

            @block.scalar
            def _(e):
                for f in ops["act"]:
                    f(e)
                self._emit_waits(e, fw["act"])

            @block.vector
            def _(e):
                for f in ops["dve"]:
                    f(e)
                self._emit_waits(e, fw["dve"])

            @block.gpsimd
            def _(e):
                for f in ops["pool"]:
                    f(e)
                self._emit_waits(e, fw["pool"])

            @block.sync
            def _(e):
                for f in ops["sp"]:
                    f(e)
                self._emit_waits(e, fw["sp"])
        sems = list(self.prog.values()) + [d.h for d in self.dsems]
        with self.nc.Block(self.name + "_clr", no_gpsimd_drain=True) as blk:
            @blk.gpsimd
            def _(e):
                for h in sems:
                    e.sem_clear(h)
        self.stack.close()


class Ring:
    def __init__(self, ph, nslots):
        self.ph = ph
        self.n = nslots
        self.t = [ph.sbuf(f"ring{i}", [128, 16, 512], BF16) for i in range(nslots)]
        self.ds = [ph.dsem(f"ringld{i}") for i in range(nslots)]
        self.count = 0
        self.free_tok = {}

    def load(self, src_ap, nk, extra_waits=()):
        i = self.count
        self.count += 1
        s = i % self.n
        waits = list(extra_waits)
        if i >= self.n:
            waits.append(self.free_tok[i - self.n])
        tok = self.ph.dma("sp", self.t[s][:, 0:nk, :], src_ap.rearrange("(k p) n -> p k n", p=128), self.ds[s], waits)
        return self.t[s], tok, i

    def release(self, idx, tok):
        self.free_tok[idx] = tok


class Prog:
    def __init__(self, cfg):
        self.cfg = cfg
        c = cfg
        nc = bass.Bass("TRN2", target_bir_lowering=False)
        self.nc = nc
        D, DFF, SL = c.D, c.DFF, c.SL
        na, ncv, dep = c.n_attn, c.n_conv, c.depth

        def din(name, shape, dt=F32):
            return nc.dram_tensor(name, list(shape), dt, kind="ExternalInput").ap()

        def dsc(name, shape, dt):
            return nc.dram_tensor(name, list(shape), dt).ap()

        self.x = din("x", [SL, D])
        self.w32 = {
            "w_qkv": din("w_qkv", [na, D, 3 * D]),
            "w_o_attn": din("w_o_attn", [na, D, D]),
            "w_bch": din("w_bch", [ncv, D, 3 * D]),
            "w_o_conv": din("w_o_conv", [ncv, D, D]),
            "w_gate": din("w_gate", [dep, D, DFF]),
            "w_up": din("w_up", [dep, D, DFF]),
            "w_down": din("w_down", [dep, DFF, D]),
        }
        self.wb = {k: dsc(k + "_bf", v.shape, BF16) for k, v in self.w32.items()}
        self.gains = din("gains_bc", [na + ncv + dep + 1, 128, D])
        self.lam_bc = din("lam_bc", [na, 4, 128, 128])
        self.subg_bc = din("subg_bc", [na, 128, 256])
        self.convw_t = din("convw_t", [ncv, 128, c.KD, 3])
        self.rope_cos = din("rope_cos", [SL, 512])
        self.rope_sin = din("rope_sin", [SL, 512])
        self.ident = din("ident", [128, 128])
        self.dmask = din("dmask", [128, 4, 512])
        self.ctxb = din("ctxb", [128, 1])
        self.hflag = din("hflag", [128, 1])
        self.out = nc.dram_tensor("out", [SL - c.out_lo * 512, D], F32, kind="ExternalOutput").ap()
        self.hres = dsc("hres", [SL, D], F32)
        self.qT = dsc("qT", [D, SL], BF16)
        self.kT = dsc("kT", [D, SL], BF16)
        self.v = dsc("v_sc", [SL, D], BF16)
        self.osc = dsc("o_sc", [SL, D], BF16)
        self.cv_tok = {}
        self.top = ExitStack()
        self.cv_sems = {}

    def phase_setup(self):
        c = self.cfg
        nc = self.nc
        top = self.top
        self.ident_bf = top.enter_context(nc.sbuf_tensor("ident_bf", [128, 128], BF16))
        self.dmask_bf = top.enter_context(nc.sbuf_tensor("dmask_bf", [128, 4, 512], BF16))
        self.ctxb_sb = top.enter_context(nc.sbuf_tensor("ctxb_sb", [128, 1], F32))
        self.hflag_sb = top.enter_context(nc.sbuf_tensor("hflag_sb", [128, 1], F32))
        self.eps_rms = top.enter_context(nc.sbuf_tensor("eps_rms", [128, 1], F32))
        self.eps_sub = top.enter_context(nc.sbuf_tensor("eps_sub", [128, 1], F32))
        ph = Phase(self, "setup")
        st_i = ph.sbuf("st_i", [128, 128], F32)
        st_m = ph.sbuf("st_m", [128, 4, 512], F32)
        d1, d2, d3, d4 = ph.dsem("d1"), ph.dsem("d2"), ph.dsem("d3"), ph.dsem("d4")
        t1 = ph.dma("sp", st_i[:], self.ident, d1)
        t2 = ph.dma("sp", st_m[:], self.dmask, d2)
        t3 = ph.dma("sp", self.ctxb_sb[:], self.ctxb, d3)
        t4 = ph.dma("sp", self.hflag_sb[:], self.hflag, d4)
        a = ph.op("dve", lambda e: e.tensor_copy(out=self.ident_bf[:], in_=st_i[:]), [t1])
        b = ph.op("dve", lambda e: e.tensor_copy(out=self.dmask_bf[:], in_=st_m[:]), [t2])
        cc = ph.op("dve", lambda e: e.memset(self.eps_rms[:], RMS_EPS))
        dd = ph.op("dve", lambda e: e.memset(self.eps_sub[:], SUBLN_EPS))
        self.cv_groups = []
        for i in range(c.depth):
            j = i // 2
            if i % 2 == 0:
                self.cv_groups.append([("w_qkv", j), ("w_o_attn", j)])
            else:
                self.cv_groups.append([("w_bch", j), ("w_o_conv", j)])
            self.cv_groups.append([("w_gate", i), ("w_up", i), ("w_down", i)])
        self.issue_conversions(ph)
        ph.finish({"sp": [t3, t4], "dve": [a, b, cc, dd]})

    def issue_conversions(self, ph):
        if not self.cv_groups:
            return
        grp = self.cv_groups.pop(0)
        for (wn, j) in grp:
            src = self.w32[wn][j]
            dst = self.wb[wn][j]
            R, C = src.shape
            h = self.top.enter_context(self.nc.semaphore(f"cv_{wn}_{j}"))
            ds = DSem(h)
            tok = None
            cw = max(d for d in range(128, 2049, 128) if C % d == 0)
            for r0 in range(0, R, 512):
                for c0 in range(0, C, cw):
                    tok = ph.dma("pool", dst[r0:r0 + 512, c0:c0 + cw], src[r0:r0 + 512, c0:c0 + cw], ds)
            self.cv_tok[(wn, j)] = tok

    def norm_a(self, ph, bufs, src_rows, gbc, gtok):
        c = self.cfg
        D = c.D
        toks = []
        for b in range(4):
            xs, xs_ds = bufs["xs"][b % 2], bufs["xs_ds"][b % 2]
            xn = bufs["xn"][b]
            ld = ph.dma("sp", xs[:], src_rows[b * 128:(b + 1) * 128, :], xs_ds, [bufs["xs_free"][b % 2]])
            ss, lnv, rstd = bufs["ss"][b % 2], bufs["lnv"][b % 2], bufs["rstd"][b % 2]
            t_sq = ph.op("act", lambda e, xs=xs, ss=ss, xn=xn: e.activation(out=xn[:], in_=xs[:], func=AF.Square, accum_out=ss[:]),
                         [ld, bufs["ss_free"][b % 2], bufs["xn_free"][b]])
            t_ln = ph.op("act", lambda e, ss=ss, lnv=lnv: e.activation(out=lnv[:], in_=ss[:], func=AF.Ln, bias=self.eps_rms[:], scale=1.0 / D),
                         [t_sq])
            t_rs = ph.op("act", lambda e, lnv=lnv, rstd=rstd: e.activation(out=rstd[:], in_=lnv[:], func=AF.Exp, scale=-0.5),
                         [t_ln])
            t_xn = ph.op("dve", lambda e, xs=xs, xn=xn, rstd=rstd: e.scalar_tensor_tensor(
                out=xn[:], in0=xs[:], scalar=rstd[:], in1=gbc[:], op0=ALU.mult, op1=ALU.mult),
                [t_rs, gtok])
            bufs["xs_free"][b % 2] = t_xn
            bufs["ss_free"][b % 2] = t_xn
            toks.append(t_xn)
        return toks

    def norm_b(self, ph, bufs, toks, hT, hT_free):
        c = self.cfg
        KD = c.KD
        last = None
        for b in range(4):
            xn = bufs["xn"][b]
            t_xn = toks[b]
            tp_tok = None
            for half in range((KD + 7) // 8):
                k0 = half * 8
                nk = min(8, KD - k0)
                bi = bufs["tp_i"] % 2
                bank = bufs["tp"][bi]
                bank_free = bufs["tp_free"][bi]
                bufs["tp_i"] += 1
                for kk in range(nk):
                    k = k0 + kk
                    w = [t_xn, bank_free] if kk == 0 else []
                    tp_tok = ph.op("pe", lambda e, xn=xn, bank=bank, k=k, kk=kk: e.transpose(
                        out=bank[:, kk * 128:(kk + 1) * 128], in_=xn[:, k * 128:(k + 1) * 128], identity=self.ident_bf[:]),
                        w, signal=(kk == nk - 1))
                ev = ph.op("act", lambda e, bank=bank, k0=k0, nk=nk, b=b: e.activation(
                    out=hT[:, k0:k0 + nk, b * 128:(b + 1) * 128],
                    in_=bank[:, 0:nk * 128].rearrange("p (k t) -> p k t", t=128), func=AF.Copy),
                    [tp_tok, hT_free])
                bufs["tp_free"][bi] = ev
                last = ev
            bufs["xn_free"][b] = tp_tok
        return last

    def norm_tile(self, ph, bufs, src_rows, gbc, gtok, hT, hT_free, tile_waits=()):
        return self.norm_b(ph, bufs, self.norm_a(ph, bufs, src_rows, gbc, gtok), hT, hT_free)

    def norm_bufs(self, ph):
        c = self.cfg
        D = c.D
        b = {}
        b["xs"] = [ph.sbuf(f"xs{i}", [128, D], F32) for i in range(2)]
        b["xs_ds"] = [ph.dsem(f"xsd{i}") for i in range(2)]
        b["xn"] = [ph.sbuf(f"xn{i}", [128, D], BF16) for i in range(4)]
        b["ss"] = [ph.sbuf(f"ss{i}", [128, 1], F32) for i in range(2)]
        b["lnv"] = [ph.sbuf(f"lnv{i}", [128, 1], F32) for i in range(2)]
        b["rstd"] = [ph.sbuf(f"rstd{i}", [128, 1], F32) for i in range(2)]
        b["tp"] = [ph.psum(f"tp{i}", [128, 1024], BF16) for i in range(2)]
        b["tp_free"] = [None, None]
        b["tp_i"] = 0
        b["xs_free"] = [None, None]
        b["ss_free"] = [None, None]
        b["xn_free"] = [None] * 4
        return b

    def load_gain(self, ph, idx):
        g = ph.sbuf("gbc", [128, self.cfg.D], F32)
        ds = ph.dsem("gbc_d")
        tok = ph.dma("sp", g[:], self.gains[idx], ds)
        return g, tok

    def proj_tm_residual(self, ph, ring, wsrc, wtok, K, lhs, lhs_tok, ps, ps_free, epi, res_src, res_dst, row0):
        c = self.cfg
        D = c.D
        kgs = [(k0, min(16, K - k0)) for k0 in range(0, K, 16)]
        last_mm = None
        stores = []
        for cg in range(D // 512):
            chunks = []
            for (k0, nk) in kgs:
                t, tok, idx = ring.load(wsrc[k0 * 128:(k0 + nk) * 128, cg * 512:(cg + 1) * 512], nk, [wtok])
                chunks.append((t, tok, idx, k0, nk))
            fin = [None] * 4
            for ci, (t, tok, idx, k0, nk) in enumerate(chunks):
                for tb in range(4):
                    for kk in range(nk):
                        k = k0 + kk
                        first = (k == 0)
                        lastk = (k == K - 1)
                        w = []
                        if kk == 0 and tb == 0:
                            w += [tok, lhs_tok]
                        if first:
                            w += [ps_free[tb]]
                        sig = lastk or (tb == 3 and kk == nk - 1)
                        mt = ph.op("pe", lambda e, t=t, tb=tb, kk=kk, k=k, first=first, lastk=lastk: e.matmul(
                            ps[tb][:], lhsT=lhs[:, k, tb * 128:(tb + 1) * 128], rhs=t[:, kk, :], start=first, stop=lastk),
                            w, signal=sig)
                        if lastk:
                            fin[tb] = mt
                        if tb == 3 and kk == nk - 1:
                            ring.release(idx, mt)
                            last_mm = mt
            for tb in range(4):
                i = epi["i"]
                epi["i"] += 1
                s = i % 4
                rp, op_ = epi["res"][s], epi["out"][s]
                rows = slice(row0 + tb * 128, row0 + (tb + 1) * 128)
                ld = ph.dma("sp", rp[:], res_src[rows, cg * 512:(cg + 1) * 512], epi["res_ds"][s], [epi["res_free"][s]])
                ad = ph.op("dve", lambda e, rp=rp, op_=op_, tb=tb: e.tensor_tensor(out=op_[:], in0=ps[tb][:], in1=rp[:], op=ALU.add),
                           [fin[tb], ld, epi["out_free"][s]])
                ps_free[tb] = ad
                epi["res_free"][s] = ad
                st = ph.dma("act", res_dst[rows, cg * 512:(cg + 1) * 512], op_[:], epi["out_ds"][s], [ad])
                epi["out_free"][s] = st
                stores.append(st)
        return last_mm, stores

    def epi_bufs(self, ph):
        e = {"i": 0}
        e["res"] = [ph.sbuf(f"resp{i}", [128, 512], F32) for i in range(4)]
        e["out"] = [ph.sbuf(f"outp{i}", [128, 512], F32) for i in range(4)]
        e["res_ds"] = [ph.dsem(f"resd{i}") for i in range(4)]
        e["out_ds"] = [ph.dsem(f"outd{i}") for i in range(4)]
        e["res_free"] = [None] * 4
        e["out_free"] = [None] * 4
        return e

    def phase_qkv(self, layer):
        c = self.cfg
        D, KD = c.D, c.KD
        j = layer // 2
        ph = Phase(self, f"qkv{layer}")
        self.issue_conversions(ph)
        ring = Ring(ph, c.ring)
        nb = self.norm_bufs(ph)
        gbc, gtok = self.load_gain(ph, j)
        hT = ph.sbuf("hT", [128, KD, 512], BF16)
        ps = [ph.psum(f"ps{i}", [128, 512], F32) for i in range(4)]
        ps_free = [None] * 4
        tq = [ph.psum(f"tq{i}", [128, 1024], BF16) for i in range(2)]
        tq_free = [None, None]
        cosb = ph.sbuf("cosb", [128, 4, 512], F32)
        sinb = ph.sbuf("sinb", [128, 4, 512], F32)
        cs_ds = ph.dsem("cs_d")
        ra = [ph.sbuf(f"ra{i}", [128, 512], F32) for i in range(2)]
        rb = [ph.sbuf(f"rb{i}", [128, 512], F32) for i in range(2)]
        rq = [ph.sbuf(f"rq{i}", [128, 512], BF16) for i in range(4)]
        rq_free = [None] * 4
        ra_free = [None, None]
        qst = [ph.sbuf(f"qst{i}", [128, 2, 512], BF16) for i in range(2)]
        qst_ds = [ph.dsem(f"qstd{i}") for i in range(2)]
        qst_free = [None, None]
        vst = [ph.sbuf(f"vst{i}", [128, 512], BF16) for i in range(4)]
        vst_ds = [ph.dsem(f"vstd{i}") for i in range(4)]
        vst_free = [None] * 4
        wsrc = self.wb["w_qkv"][j]
        wtok = self.cv_tok[("w_qkv", j)]
        src = self.x if layer == 0 else self.hres
        hT_free = None
        cs_free = None
        finals = []
        ri = 0
        qi = 0
        vi = 0
        last_rope = None
        na_toks = self.norm_a(ph, nb, src[0:512, :], gbc, gtok)
        hT_tok = self.norm_b(ph, nb, na_toks, hT, None)
        for ti in range(0, c.NT):
            r0 = ti * 512
            if ti + 1 < c.NT:
                na_toks = self.norm_a(ph, nb, src[r0 + 512:r0 + 1024, :], gbc, gtok)
            cst1 = ph.dma("sp", cosb[:], self.rope_cos[r0:r0 + 512, :].rearrange("(b p) n -> p b n", p=128), cs_ds, [cs_free])
            cst2 = ph.dma("sp", sinb[:], self.rope_sin[r0:r0 + 512, :].rearrange("(b p) n -> p b n", p=128), cs_ds, [cs_free])
            last_mm = None
            for ch in range(3 * D // 512):
                kind = ch // (D // 512)
                if kind == 0 and ti < c.lo[layer]:
                    continue
                t, ltok, idx = ring.load(wsrc[:, ch * 512:(ch + 1) * 512], KD, [wtok])
                fin = [None] * 4
                for tb in range(4):
                    for k in range(KD):
                        w = []
                        if k == 0 and tb == 0:
                            w += [ltok, hT_tok]
                        if k == 0:
                            w += [ps_free[tb]]
                        sig = (k == KD - 1)
                        mt = ph.op("pe", lambda e, t=t, tb=tb, k=k: e.matmul(
                            ps[tb][:], lhsT=hT[:, k, tb * 128:(tb + 1) * 128], rhs=t[:, k, :], start=(k == 0), stop=(k == KD - 1)),
                            w, signal=sig)
                    fin[tb] = mt
                ring.release(idx, mt)
                last_mm = mt
                if kind == 2:
                    for tb in range(4):
                        s = vi % 4
                        vi += 1
                        cp = ph.op("act", lambda e, s=s, tb=tb: e.activation(out=vst[s][:], in_=ps[tb][:], func=AF.Copy),
                                   [fin[tb], vst_free[s]])
                        ps_free[tb] = cp
                        st = ph.dma("act", self.v[r0 + tb * 128:r0 + (tb + 1) * 128, (ch * 512 - 2 * D):(ch * 512 - 2 * D) + 512],
                                    vst[s][:], vst_ds[s], [cp])
                        vst_free[s] = st
                        finals.append(st)
                else:
                    rope_toks = []
                    for tb in range(4):
                        s = ri % 2
                        ri += 1
                        A, B = ra[s], rb[s]
                        pv = ps[tb][:].rearrange("p (h x) -> p h x", x=128)
                        Bv = B[:].rearrange("p (h x) -> p h x", x=128)
                        sv = sinb[:, tb, :].rearrange("p (h x) -> p h x", x=128)
                        o1 = ph.op("dve", lambda e, A=A, tb=tb: e.tensor_tensor(out=A[:], in0=ps[tb][:], in1=cosb[:, tb, :], op=ALU.mult),
                                   [fin[tb], cst1, cst2, ra_free[s]])
                        o2 = ph.op("dve", lambda e, pv=pv, Bv=Bv, sv=sv: e.tensor_tensor(
                            out=Bv[:, :, 0:64], in0=pv[:, :, 64:128], in1=sv[:, :, 0:64], op=ALU.mult), [o1])
                        o3 = ph.op("dve", lambda e, pv=pv, Bv=Bv, sv=sv: e.tensor_tensor(
                            out=Bv[:, :, 64:128], in0=pv[:, :, 0:64], in1=sv[:, :, 64:128], op=ALU.mult), [o2])
                        ps_free[tb] = o3
                        o4 = ph.op("dve", lambda e, A=A, B=B, tb=tb: e.tensor_tensor(out=rq[tb][:], in0=A[:], in1=B[:], op=ALU.add),
                                   [o3, rq_free[tb]])
                        ra_free[s] = o4
                        rope_toks.append(o4)
                        last_rope = o4
                    dstT = self.qT if kind == 0 else self.kT
                    head0 = (ch % (D // 512)) * 4
                    for hp in range(2):
                        bi = qi % 2
                        qi += 1
                        bank = tq[bi]
                        tt = None
                        for hh in range(2):
                            hd = hp * 2 + hh
                            for tb in range(4):
                                w = [rope_toks[tb]]
                                if hh == 0 and tb == 0:
                                    w.append(tq_free[bi])
                                lastt = (hh == 1 and tb == 3)
                                tt = ph.op("pe", lambda e, bank=bank, hh=hh, tb=tb, hd=hd: e.transpose(
                                    out=bank[:, hh * 512 + tb * 128: hh * 512 + (tb + 1) * 128],
                                    in_=rq[tb][:, hd * 128:(hd + 1) * 128], identity=self.ident_bf[:]),
                                    w, signal=lastt or (hp == 1 and hh == 1))
                                if hp == 1 and hh == 1:
                                    rq_free[tb] = tt
                        ev = ph.op("act", lambda e, bank=bank, bi=bi: e.activation(
                            out=qst[bi][:].rearrange("p a t -> p (a t)"), in_=bank[:], func=AF.Copy),
                            [tt, qst_free[bi]])
                        tq_free[bi] = ev
                        rows = slice((head0 + hp * 2) * 128, (head0 + hp * 2 + 2) * 128)
                        st = ph.dma("act", dstT[rows, r0:r0 + 512].rearrange("(a p) t -> p a t", p=128), qst[bi][:], qst_ds[bi], [ev])
                        qst_free[bi] = st
                        finals.append(st)
            hT_free = last_mm
            if ti + 1 < c.NT:
                hT_tok = self.norm_b(ph, nb, na_toks, hT, hT_free)
            cs_free = last_rope
        ph.finish({"act": finals[-40:], "sp": [], "dve": []})

    def phase_att(self, layer):
        c = self.cfg
        D, SL, NT = c.D, c.SL, c.NT
        j = layer // 2
        lam_init = 0.8 - 0.6 * math.exp(-0.3 * layer)
        scale = 128 ** -0.5
        ph = Phase(self, f"att{layer}")
        self.issue_conversions(ph)
        NKB = SL // 128
        KT = [[ph.sbuf(f"KT{b}_{cc}", [128, SL], BF16) for cc in range(2)] for b in range(2)]
        KT_ds = [[ph.dsem(f"KTd{b}_{cc}") for cc in range(2)] for b in range(2)]
        VA = [ph.sbuf(f"VA{b}", [128, NKB, 258], BF16) for b in range(2)]
        VA_ds = [ph.dsem(f"VAd{b}") for b in range(2)]
        QT = [[ph.sbuf(f"QT{b}_{cc}", [128, 512], BF16) for cc in range(2)] for b in range(2)]
        QT_ds = [[ph.dsem(f"QTd{b}_{cc}") for cc in range(2)] for b in range(2)]
        PT = [ph.sbuf(f"PT{i}", [128, 512], BF16) for i in range(3)]
        psS = [ph.psum(f"psS{i}", [128, 512], F32) for i in range(2)]
        psO = [ph.psum(f"psO{i}", [128, 512], F32) for i in range(4)]
        t1 = [ph.sbuf(f"t1_{i}", [128, 256], F32) for i in range(4)]
        ocb = [ph.sbuf(f"ocb{i}", [128, 256], F32) for i in range(2)]
        obf = [ph.sbuf(f"obf{i}", [128, 256], BF16) for i in range(4)]
        obf_ds = [ph.dsem(f"obfd{i}") for i in range(4)]
        sm = {n: [ph.sbuf(f"{n}{i}", [128, 1], F32) for i in range(2)] for n in ("r1", "r2", "ss", "lnv", "rs")}
        junk = ph.sbuf("junk", [128, 256], BF16)
        lt = [ph.sbuf(f"lt{i}", [128, 128], F32) for i in range(4)]
        lds = [ph.dsem(f"ltd{i}") for i in range(4)]
        lj = ph.sbuf("lj", [128, 128], F32)
        ls = [ph.sbuf(f"ls{i}", [128, 1], F32) for i in range(2)]
        le = [ph.sbuf(f"le{i}", [128, 1], F32) for i in range(2)]
        neglam = ph.sbuf("neglam", [128, 1], F32)
        subg = ph.sbuf("subg", [128, 256], F32)
        sg_ds = ph.dsem("sgd")
        ltok = [ph.dma("sp", lt[i][:], self.lam_bc[j, i], lds[i]) for i in range(4)]
        sgtok = ph.dma("sp", subg[:], self.subg_bc[j], sg_ds)
        a1 = ph.op("dve", lambda e: e.scalar_tensor_tensor(out=lj[:], in0=lt[0][:], scalar=1.0, in1=lt[1][:],
                                                            op0=ALU.mult, op1=ALU.mult, accum_out=ls[0][:]), [ltok[0], ltok[1]])
        a2 = ph.op("dve", lambda e: e.scalar_tensor_tensor(out=lj[:], in0=lt[2][:], scalar=1.0, in1=lt[3][:],
                                                            op0=ALU.mult, op1=ALU.mult, accum_out=ls[1][:]), [ltok[2], ltok[3], a1])
        e1 = ph.op("act", lambda e: e.activation(out=le[0][:], in_=ls[0][:], func=AF.Exp), [a1])
        e2 = ph.op("act", lambda e: e.activation(out=le[1][:], in_=ls[1][:], func=AF.Exp), [a2])
        a3 = ph.op("dve", lambda e: e.tensor_tensor(out=neglam[:], in0=le[1][:], in1=le[0][:], op=ALU.subtract), [e1, e2])
        a4 = ph.op("dve", lambda e: e.tensor_scalar(out=neglam[:], in0=neglam[:], scalar1=-lam_init, scalar2=None, op0=ALU.add), [a3])
        a5 = ph.op("dve", lambda e: e.tensor_scalar(out=subg[:], in0=subg[:], scalar1=(1.0 - lam_init), scalar2=None, op0=ALU.mult), [sgtok])
        ones_tok = [ph.op("dve", lambda e, b=b: e.memset(VA[b][:, :, 256:257], 1.0)) for b in range(2)]
        const_tok = [a4, a5]

        KT_free = [[None, None], [None, None]]
        VA_free = [None, None]
        QT_free = [[None, None], [None, None]]
        PT_free = [None] * 3
        psS_free = [None, None]
        psO_free = [None] * 4
        t1_free = [None] * 4
        ocb_free = [None, None]
        obf_free = [None] * 4
        sm_free = [None, None]
        junk_tok = None
        finals = []
        si = 0
        qi = 0
        oi = 0
        ei = 0
        for h in range(c.NH):
            hb = h % 2
            ktok = []
            for cc in range(2):
                comp = 2 * h + cc
                ktok.append(ph.dma("sp", KT[hb][cc][:], self.kT[comp * 128:(comp + 1) * 128, :], KT_ds[hb][cc], [KT_free[hb][cc]]))
            vtok = ph.dma("sp", VA[hb][:, :, 0:256], self.v[:, h * 256:(h + 1) * 256].rearrange("(b p) e -> p b e", p=128),
                          VA_ds[hb], [VA_free[hb], ones_tok[hb]])
            last_pv_head = None
            last_s_head = [None, None]
            for Q in range(c.lo[layer], NT):
                qb_ = qi % 2
                qi += 1
                qtok = []
                for cc in range(2):
                    comp = 2 * h + cc
                    qtok.append(ph.dma("sp", QT[qb_][cc][:], self.qT[comp * 128:(comp + 1) * 128, Q * 512:(Q + 1) * 512],
                                       QT_ds[qb_][cc], [QT_free[qb_][cc]]))
                for cc in range(2):
                    nkb = 4 * Q + 4
                    s_toks = {}
                    p_toks = {}

                    def emit_S(kb):
                        nonlocal si
                        sb_ = si % 2
                        si += 1
                        w = [psS_free[sb_]]
                        if kb == 0:
                            w += [ktok[cc], qtok[cc]]
                        tk = ph.op("pe", lambda e, sb_=sb_, kb=kb, kt=KT[hb][cc], qt=QT[qb_][cc]: e.matmul(
                            psS[sb_][:], lhsT=kt[:, kb * 128:(kb + 1) * 128], rhs=qt[:], start=True, stop=True), w)
                        s_toks[kb] = (tk, sb_)
                        return tk

                    def emit_exp(kb):
                        nonlocal ei
                        tk, sb_ = s_toks[kb]
                        pb = ei % 3
                        ei += 1
                        use_ctx = (Q >= c.CTXT and kb < c.CTXT * 4)
                        if use_ctx:
                            f = lambda e, sb_=sb_, pb=pb: e.activation(out=PT[pb][:], in_=psS[sb_][:], func=AF.Exp,
                                                                        bias=self.ctxb_sb[:], scale=scale)
                        else:
                            f = lambda e, sb_=sb_, pb=pb: e.activation(out=PT[pb][:], in_=psS[sb_][:], func=AF.Exp, scale=scale)
                        ex = ph.op("act", f, [tk, PT_free[pb]])
                        psS_free[sb_] = ex
                        v = kb - 4 * Q
                        if v >= 0:
                            ex = ph.op("dve", lambda e, pb=pb, v=v: e.tensor_tensor(
                                out=PT[pb][:], in0=PT[pb][:], in1=self.dmask_bf[:, v, :], op=ALU.mult), [ex])
                        p_toks[kb] = (ex, pb)

                    def emit_PV(kb):
                        ex, pb = p_toks[kb]
                        v = kb - 4 * Q
                        lastt = None
                        for jq in range(4):
                            if v >= 0 and jq < v:
                                continue
                            w = []
                            if lastt is None:
                                w += [ex]
                                if kb == 0:
                                    w += [vtok]
                            if kb == 0:
                                w += [psO_free[jq]]
                            stop = (kb == 4 * Q + jq)
                            lastt = ph.op("pe", lambda e, jq=jq, pb=pb, kb=kb, stop=stop, va=VA[hb]: e.matmul(
                                psO[jq][:, 0:257], lhsT=PT[pb][:, jq * 128:(jq + 1) * 128], rhs=va[:, kb, 0:257],
                                start=(kb == 0), stop=stop), w, signal=(stop or jq == 3))
                            if stop:
                                o_fin[jq] = lastt
                        PT_free[pb] = lastt
                        return lastt

                    o_fin = [None] * 4
                    emit_S(0)
                    lastpv = None
                    for kb in range(nkb):
                        if kb + 1 < nkb:
                            emit_S(kb + 1)
                        emit_exp(kb)
                        lastpv = emit_PV(kb)
                    last_pv_head = lastpv
                    last_s_head[cc] = s_toks[nkb - 1][0]
                    for jq in range(4):
                        m = oi % 2
                        if cc == 0:
                            r = ph.op("dve", lambda e, jq=jq, m=m: e.reciprocal(out=sm["r1"][m][:], in_=psO[jq][:, 256:257]),
                                      [o_fin[jq], sm_free[m]])
                            tt = ph.op("dve", lambda e, jq=jq, m=m: e.tensor_scalar(
                                out=t1[jq][:], in0=psO[jq][:, 0:256], scalar1=sm["r1"][m][:], scalar2=None, op0=ALU.mult),
                                [r, t1_free[jq]])
                            psO_free[jq] = tt
                            sm_free[m] = tt
                            oi += 1
                        else:
                            r = ph.op("dve", lambda e, jq=jq, m=m: e.reciprocal(out=sm["r2"][m][:], in_=psO[jq][:, 256:257]),
                                      [o_fin[jq], sm_free[m]] + const_tok)
                            r2 = ph.op("dve", lambda e, m=m: e.tensor_tensor(out=sm["r2"][m][:], in0=sm["r2"][m][:], in1=neglam[:], op=ALU.mult), [r])
                            oc = ph.op("dve", lambda e, jq=jq, m=m: e.scalar_tensor_tensor(
                                out=ocb[m][:], in0=psO[jq][:, 0:256], scalar=sm["r2"][m][:], in1=t1[jq][:], op0=ALU.mult, op1=ALU.add),
                                [r2, ocb_free[m]])
                            psO_free[jq] = oc
                            t1_free[jq] = oc
                            sq = ph.op("act", lambda e, m=m: e.activation(out=junk[:], in_=ocb[m][:], func=AF.Square, accum_out=sm["ss"][m][:]),
                                       [oc, junk_tok])
                            junk_tok = sq
                            ln = ph.op("act", lambda e, m=m: e.activation(out=sm["lnv"][m][:], in_=sm["ss"][m][:], func=AF.Ln,
                                                                           bias=self.eps_sub[:], scale=1.0 / 256), [sq])
                            rs = ph.op("act", lambda e, m=m: e.activation(out=sm["rs"][m][:], in_=sm["lnv"][m][:], func=AF.Exp, scale=-0.5), [ln])
                            ob = oi % 4
                            fo = ph.op("dve", lambda e, m=m, ob=ob: e.scalar_tensor_tensor(
                                out=obf[ob][:], in0=ocb[m][:], scalar=sm["rs"][m][:], in1=subg[:], op0=ALU.mult, op1=ALU.mult),
                                [rs, obf_free[ob]])
                            ocb_free[m] = fo
                            sm_free[m] = fo
                            rows = slice(Q * 512 + jq * 128, Q * 512 + (jq + 1) * 128)
                            st = ph.dma("act", self.osc[rows, h * 256:(h + 1) * 256], obf[ob][:], obf_ds[ob], [fo])
                            obf_free[ob] = st
                            finals.append(st)
                            oi += 1
                    QT_free[qb_][cc] = s_toks[nkb - 1][0]
            for cc in range(2):
                KT_free[hb][cc] = last_s_head[cc]
            VA_free[hb] = last_pv_head
        ph.finish({"act": finals[-8:]})

    def phase_oproj(self, layer):
        c = self.cfg
        D, KD = c.D, c.KD
        j = layer // 2
        ph = Phase(self, f"opj{layer}")
        self.issue_conversions(ph)
        ring = Ring(ph, c.ring)
        epi = self.epi_bufs(ph)
        oT = ph.sbuf("oT", [128, KD, 512], BF16)
        ob = [ph.sbuf(f"ob{i}", [128, D], BF16) for i in range(2)]
        ob_ds = [ph.dsem(f"obd{i}") for i in range(2)]
        ob_free = [None, None]
        tp = [ph.psum(f"tp{i}", [128, 1024], BF16) for i in range(2)]
        tp_free = [None, None]
        ps = [ph.psum(f"ps{i}", [128, 512], F32) for i in range(4)]
        ps_free = [None] * 4
        oT_free = None
        res_src = self.x if layer == 0 else self.hres
        allst = []
        tpi = 0
        for ti in range(c.lo[layer], c.NT):
            r0 = ti * 512
            last_ev = None
            for b in range(4):
                s = b % 2
                ld = ph.dma("sp", ob[s][:], self.osc[r0 + b * 128:r0 + (b + 1) * 128, :], ob_ds[s], [ob_free[s]])
                tt = None
                for half in range((KD + 7) // 8):
                    k0 = half * 8
                    nk = min(8, KD - k0)
                    bi = tpi % 2
                    tpi += 1
                    for kk in range(nk):
                        k = k0 + kk
                        w = [ld, tp_free[bi]] if kk == 0 else []
                        tt = ph.op("pe", lambda e, s=s, bi=bi, k=k, kk=kk: e.transpose(
                            out=tp[bi][:, kk * 128:(kk + 1) * 128], in_=ob[s][:, k * 128:(k + 1) * 128], identity=self.ident_bf[:]),
                            w, signal=(kk == nk - 1))
                    ev = ph.op("act", lambda e, bi=bi, k0=k0, nk=nk, b=b: e.activation(
                        out=oT[:, k0:k0 + nk, b * 128:(b + 1) * 128],
                        in_=tp[bi][:, 0:nk * 128].rearrange("p (k t) -> p k t", t=128), func=AF.Copy), [tt, oT_free])
                    tp_free[bi] = ev
                    last_ev = ev
                ob_free[s] = tt
            last_mm, sts = self.proj_tm_residual(ph, ring, self.wb["w_o_attn"][j], self.cv_tok[("w_o_attn", j)], KD,
                                                 oT, last_ev, ps, ps_free, epi, res_src, self.hres, r0)
            oT_free = last_mm
            allst += sts
        ph.finish({"act": allst[-8:]})

    def phase_ffn(self, layer):
        c = self.cfg
        D, KD, KF = c.D, c.KD, c.KF
        ph = Phase(self, f"ffn{layer}")
        self.issue_conversions(ph)
        ring = Ring(ph, c.ring)
        nb = self.norm_bufs(ph)
        epi = self.epi_bufs(ph)
        gbc, gtok = self.load_gain(ph, c.n_attn + c.n_conv + layer)
        hT = ph.sbuf("hT", [128, KD, 512], BF16)
        actT = ph.sbuf("actT", [128, KF, 512], BF16)
        pg = [ph.psum(f"pg{i}", [128, 512], F32) for i in range(3)]
        pu = [ph.psum(f"pu{i}", [128, 512], F32) for i in range(3)]
        pp_free = [None] * 3
        sg = [ph.sbuf(f"sg{i}", [128, 512], F32) for i in range(2)]
        sg_free = [None, None]
        ps_free = [None] * 4
        hT_free = None
        actT_free = None
        wg, wu, wd = self.wb["w_gate"][layer], self.wb["w_up"][layer], self.wb["w_down"][layer]
        tg, tu, td = self.cv_tok[("w_gate", layer)], self.cv_tok[("w_up", layer)], self.cv_tok[("w_down", layer)]
        allst = []
        pi = 0
        tiles = list(range(c.lo[layer], c.NT))
        na_toks = self.norm_a(ph, nb, self.hres[tiles[0] * 512:tiles[0] * 512 + 512, :], gbc, gtok)
        hT_tok = self.norm_b(ph, nb, na_toks, hT, None)
        for tix, ti in enumerate(tiles):
            r0 = ti * 512
            nxt = tiles[tix + 1] if tix + 1 < len(tiles) else None
            if nxt is not None:
                na_toks = self.norm_a(ph, nb, self.hres[nxt * 512:nxt * 512 + 512, :], gbc, gtok)
            last_mm = None
            last_act = None
            for i in range(c.DFF // 512):
                gt, gl, gi = ring.load(wg[:, i * 512:(i + 1) * 512], KD, [tg])
                ut, ul, ui = ring.load(wu[:, i * 512:(i + 1) * 512], KD, [tu])
                for m in range(4):
                    p = pi % 3
                    pi += 1
                    mg = None
                    for k in range(KD):
                        w = []
                        if k == 0:
                            w += [pp_free[p]]
                            if m == 0:
                                w += [gl, hT_tok]
                        mg = ph.op("pe", lambda e, p=p, gt=gt, m=m, k=k: e.matmul(
                            pg[p][:], lhsT=gt[:, k, m * 128:(m + 1) * 128], rhs=hT[:, k, :], start=(k == 0), stop=(k == KD - 1)),
                            w, signal=(k == KD - 1))
                    mu = None
                    for k in range(KD):
                        w = [ul] if (k == 0 and m == 0) else []
                        mu = ph.op("pe", lambda e, p=p, ut=ut, m=m, k=k: e.matmul(
                            pu[p][:], lhsT=ut[:, k, m * 128:(m + 1) * 128], rhs=hT[:, k, :], start=(k == 0), stop=(k == KD - 1)),
                            w, signal=(k == KD - 1))
                    s = pi % 2
                    sl = ph.op("act", lambda e, p=p, s=s: e.activation(out=sg[s][:], in_=pg[p][:], func=AF.Silu), [mg, sg_free[s]])
                    ml = ph.op("dve", lambda e, p=p, s=s, i=i, m=m: e.tensor_tensor(
                        out=actT[:, i * 4 + m, :], in0=sg[s][:], in1=pu[p][:], op=ALU.mult), [sl, mu, actT_free])
                    sg_free[s] = ml
                    pp_free[p] = ml
                    last_act = ml
                ring.release(gi, mu)
                ring.release(ui, mu)
                last_mm = mu
            hT_free = last_mm
            cur_hT_tok = hT_tok
            if nxt is not None:
                hT_tok = self.norm_b(ph, nb, na_toks, hT, hT_free)
            ps = [pg[0], pg[1], pu[0], pu[1]]
            ps_free = [last_act] * 4
            lmm, sts = self.proj_tm_residual(ph, ring, wd, td, KF, actT, last_act, ps, ps_free, epi, self.hres, self.hres, r0)
            actT_free = lmm
            pp_free = [ps_free[3]] * 3
            allst += sts
        ph.finish({"act": allst[-8:]})

    def phase_conv(self, layer):
        c = self.cfg
        D, KD = c.D, c.KD
        j = layer // 2
        ph = Phase(self, f"cnv{layer}")
        self.issue_conversions(ph)
        ring = Ring(ph, c.ring)
        nb = self.norm_bufs(ph)
        epi = self.epi_bufs(ph)
        gbc, gtok = self.load_gain(ph, c.n_attn + j)
        hT = ph.sbuf("hT", [128, KD, 512], BF16)
        yT = ph.sbuf("yT", [128, KD, 512], BF16)
        cw = ph.sbuf("cw", [128, KD, 3], F32)
        cw_ds = ph.dsem("cwd")
        cwtok = ph.dma("sp", cw[:], self.convw_t[j], cw_ds)
        halo = ph.sbuf("halo", [128, KD, 2], F32)
        hz = ph.op("dve", lambda e: e.memset(halo[:], 0.0))
        pb_ = [ph.psum(f"pb{i}", [128, 512], F32) for i in range(2)]
        pc_ = [ph.psum(f"pc{i}", [128, 512], F32) for i in range(2)]
        pu_ = [ph.psum(f"pu{i}", [128, 512], F32) for i in range(2)]
        pp_free = [None, None]
        gcs = [ph.sbuf(f"gcs{i}", [128, 512], F32) for i in range(2)]
        zb = [ph.sbuf(f"zb{i}", [128, 514], F32) for i in range(2)]
        zc = [ph.sbuf(f"zc{i}", [128, 512], F32) for i in range(2)]
        buf_free = [None, None]
        wsrc = self.wb["w_bch"][j]
        wtok = self.cv_tok[("w_bch", j)]
        ps_free = [None] * 4
        hT_free = None
        yT_free = None
        halo_tok = {m: hz for m in range(KD)}
        allst = []
        pi = 0
        nD = D // 512
        tiles = list(range(c.lo[layer], c.NT))
        na_toks = self.norm_a(ph, nb, self.hres[tiles[0] * 512:tiles[0] * 512 + 512, :], gbc, gtok)
        hT_tok = self.norm_b(ph, nb, na_toks, hT, None)
        for tix, ti in enumerate(tiles):
            r0 = ti * 512
            nxt = tiles[tix + 1] if tix + 1 < len(tiles) else None
            if nxt is not None:
                na_toks = self.norm_a(ph, nb, self.hres[nxt * 512:nxt * 512 + 512, :], gbc, gtok)
            if ti == c.CTXT:
                hf = ph.op("dve", lambda e: e.tensor_scalar(out=halo[:], in0=halo[:], scalar1=self.hflag_sb[:], scalar2=None, op0=ALU.mult),
                           list(halo_tok.values()))
                halo_tok = {m: hf for m in range(KD)}
            last_mm = None
            last_y = None
            for i in range(nD):
                bt, bl, bi_ = ring.load(wsrc[:, i * 512:(i + 1) * 512], KD, [wtok])
                ct, cl, ci_ = ring.load(wsrc[:, D + i * 512:D + (i + 1) * 512], KD, [wtok])
                ut, ul, ui_ = ring.load(wsrc[:, 2 * D + i * 512:2 * D + (i + 1) * 512], KD, [wtok])
                for m in range(4):
                    p = pi % 2
                    pi += 1
                    mm = {}
                    for nm, wt, lt_, pst in (("b", bt, bl, pb_), ("c", ct, cl, pc_), ("u", ut, ul, pu_)):
                        for k in range(KD):
                            w = []
                            if k == 0 and nm == "b":
                                w += [pp_free[p]]
                            if k == 0 and m == 0:
                                w += [lt_, hT_tok]
                            mm[nm] = ph.op("pe", lambda e, pst=pst, p=p, wt=wt, m=m, k=k: e.matmul(
                                pst[p][:], lhsT=wt[:, k, m * 128:(m + 1) * 128], rhs=hT[:, k, :], start=(k == 0), stop=(k == KD - 1)),
                                w, signal=(k == KD - 1))
                    fm = i * 4 + m
                    g1 = ph.op("act", lambda e, p=p: e.activation(out=gcs[p][:], in_=pc_[p][:], func=AF.Copy), [mm["c"], buf_free[p]])
                    z1 = ph.op("dve", lambda e, p=p: e.tensor_tensor(out=zb[p][:, 2:514], in0=gcs[p][:], in1=pu_[p][:], op=ALU.mult),
                               [g1, mm["u"]])
                    z0 = ph.op("dve", lambda e, p=p, fm=fm: e.tensor_copy(out=zb[p][:, 0:2], in_=halo[:, fm, :]), [halo_tok[fm], z1, cwtok])
                    c1 = ph.op("dve", lambda e, p=p, fm=fm: e.tensor_scalar(
                        out=zc[p][:], in0=zb[p][:, 0:512], scalar1=cw[:, fm, 0:1], scalar2=None, op0=ALU.mult), [z0])
                    c2 = ph.op("dve", lambda e, p=p, fm=fm: e.scalar_tensor_tensor(
                        out=zc[p][:], in0=zb[p][:, 1:513], scalar=cw[:, fm, 1:2], in1=zc[p][:], op0=ALU.mult, op1=ALU.add), [c1])
                    c3 = ph.op("dve", lambda e, p=p, fm=fm: e.scalar_tensor_tensor(
                        out=zc[p][:], in0=zb[p][:, 2:514], scalar=cw[:, fm, 2:3], in1=zc[p][:], op0=ALU.mult, op1=ALU.add), [c2])
                    hn = ph.op("dve", lambda e, p=p, fm=fm: e.tensor_copy(out=halo[:, fm, :], in_=zb[p][:, 512:514]), [c3])
                    halo_tok[fm] = hn
                    y1 = ph.op("dve", lambda e, p=p, fm=fm: e.tensor_tensor(out=yT[:, fm, :], in0=zc[p][:], in1=pb_[p][:], op=ALU.mult),
                               [hn, mm["b"], yT_free])
                    pp_free[p] = y1
                    buf_free[p] = y1
                    last_y = y1
                for idx in (bi_, ci_, ui_):
                    ring.release(idx, mm["u"])
                last_mm = mm["u"]
            hT_free = last_mm
            if nxt is not None:
                hT_tok = self.norm_b(ph, nb, na_toks, hT, hT_free)
            ps = [pb_[0], pb_[1], pc_[0], pc_[1]]
            ps_free = [last_y] * 4
            lmm, sts = self.proj_tm_residual(ph, ring, self.wb["w_o_conv"][j], self.cv_tok[("w_o_conv", j)], KD,
                                             yT, last_y, ps, ps_free, epi, self.hres, self.hres, r0)
            yT_free = lmm
            pp_free = [ps_free[3], ps_free[3]]
            allst += sts
        ph.finish({"act": allst[-8:]})

    def phase_final(self):
        c = self.cfg
        D = c.D
        ph = Phase(self, "final")
        gbc, gtok = self.load_gain(ph, c.n_attn + c.n_conv + c.depth)
        xs = [ph.sbuf(f"xs{i}", [128, D], F32) for i in range(2)]
        xs_ds = [ph.dsem(f"xsd{i}") for i in range(2)]
        yo = [ph.sbuf(f"yo{i}", [128, D], F32) for i in range(2)]
        yo_ds = [ph.dsem(f"yod{i}") for i in range(2)]
        ss = [ph.sbuf(f"ss{i}", [128, 1], F32) for i in range(2)]
        lnv = [ph.sbuf(f"lnv{i}", [128, 1], F32) for i in range(2)]
        rs = [ph.sbuf(f"rs{i}", [128, 1], F32) for i in range(2)]
        junk = ph.sbuf("junk", [128, D], BF16)
        xs_free = [None, None]
        yo_free = [None, None]
        jt = None
        sts = []
        nblk = (c.NT - c.out_lo) * 4
        for b in range(nblk):
            s = b % 2
            r0 = c.out_lo * 512 + b * 128
            ld = ph.dma("sp", xs[s][:], self.hres[r0:r0 + 128, :], xs_ds[s], [xs_free[s]])
            sq = ph.op("act", lambda e, s=s: e.activation(out=junk[:], in_=xs[s][:], func=AF.Square, accum_out=ss[s][:]), [ld, jt])
            jt = sq
            ln = ph.op("act", lambda e, s=s: e.activation(out=lnv[s][:], in_=ss[s][:], func=AF.Ln, bias=self.eps_rms[:], scale=1.0 / D), [sq])
            rr = ph.op("act", lambda e, s=s: e.activation(out=rs[s][:], in_=lnv[s][:], func=AF.Exp, scale=-0.5), [ln])
            yy = ph.op("dve", lambda e, s=s: e.scalar_tensor_tensor(out=yo[s][:], in0=xs[s][:], scalar=rs[s][:], in1=gbc[:],
                                                                     op0=ALU.mult, op1=ALU.mult), [rr, gtok, yo_free[s]])
            xs_free[s] = yy
            st = ph.dma("act", self.out[b * 128:(b + 1) * 128, :], yo[s][:], yo_ds[s], [yy])
            yo_free[s] = st
            sts.append(st)
        ph.finish({"act": sts[-2:]})

    def build(self):
        c = self.cfg
        self.phase_setup()
        for i in range(c.depth):
            if i % 2 == 0:
                self.phase_qkv(i)
                self.phase_att(i)
                self.phase_oproj(i)
            else:
                self.phase_conv(i)
            self.phase_ffn(i)
        self.phase_final()
        self.top.close()
        return self.nc


def host_tables(cfg, half):
    SL, CT = cfg.SL, cfg.CTXT * 512
    pos = np.arange(SL, dtype=np.float32)
    if half == 0:
        pos = np.maximum(pos - CT, 0.0).astype(np.float32)
    inv_freq = (1.0 / (10000.0 ** (np.arange(0, 128, 2, dtype=np.float32) / 128.0))).astype(np.float32)
    ang = pos[:, None] * inv_freq[None, :]
    cos, sin = np.cos(ang).astype(np.float32), np.sin(ang).astype(np.float32)
    cosF = np.concatenate([cos, cos], axis=1)
    sinF = np.concatenate([-sin, sin], axis=1)
    rope_cos = np.ascontiguousarray(np.tile(cosF, (1, 4)))
    rope_sin = np.ascontiguousarray(np.tile(sinF, (1, 4)))
    kp = np.arange(128)[:, None, None]
    v = np.arange(4)[None, :, None]
    qf = np.arange(512)[None, None, :]
    dmask = (v * 128 + kp <= qf).astype(np.float32)
    ctxb = np.full((128, 1), 0.0 if half == 1 else NEG_BIG, np.float32)
    hflag = np.full((128, 1), 1.0 if half == 1 else 0.0, np.float32)
    return dict(rope_cos=rope_cos, rope_sin=rope_sin, dmask=dmask, ctxb=ctxb, hflag=hflag,
                ident=np.eye(128, dtype=np.float32))


def host_params(cfg, inp):
    f = np.float32
    def bc(a):
        a = np.asarray(a, f)
        return np.ascontiguousarray(np.broadcast_to(a[..., None, :], a.shape[:-1] + (128, a.shape[-1])))
    gains = np.concatenate([np.asarray(inp["attn_norm_g"], f), np.asarray(inp["conv_norm_g"], f),
                            np.asarray(inp["ffn_norm_g"], f), np.asarray(inp["final_norm_g"], f)[None]], axis=0)
    lam = np.stack([np.asarray(inp[k], f) for k in ("lambda_q1", "lambda_k1", "lambda_q2", "lambda_k2")], axis=1)
    convw = np.asarray(inp["conv_w"], f)
    ncv = convw.shape[0]
    convw_t = np.ascontiguousarray(convw.reshape(ncv, 3, cfg.KD, 128).transpose(0, 3, 2, 1))
    d = dict(gains_bc=bc(gains), lam_bc=bc(lam), subg_bc=bc(np.asarray(inp["subln_g"], f)), convw_t=convw_t)
    for k in ("w_qkv", "w_o_attn", "w_bch", "w_o_conv", "w_gate", "w_up", "w_down"):
        d[k] = np.ascontiguousarray(np.asarray(inp[k], f))
    return d


_NC_CACHE = {}


def run_model(cfg, inp, n_cores=None):
    key = (cfg.D, cfg.DFF, cfg.SL, cfg.depth, cfg.mode)
    if key not in _NC_CACHE:
        _NC_CACHE[key] = Prog(cfg).build()
    nc = _NC_CACHE[key]
    x = np.asarray(inp["x"], np.float32)
    B, S, D = x.shape
    CT = cfg.CTXT * 512
    params = host_params(cfg, inp)
    tabs = [host_tables(cfg, 0), host_tables(cfg, 1)]
    in_maps = []
    if cfg.mode == "A":
        n_cores = B if n_cores is None else n_cores
        for c in range(n_cores):
            m = dict(params)
            m.update(tabs[1])
            m["x"] = np.ascontiguousarray(x[c % B])
            in_maps.append(m)
        res = run_bass_kernel_spmd(nc, in_maps, core_ids=list(range(n_cores)))
        return np.stack([res.results[b]["out"] for b in range(B)], axis=0).astype(np.float32)
    n_cores = 2 * B if n_cores is None else n_cores
    for c in range(n_cores):
        b, half = (c // 2) % B, c % 2
        if half == 1:
            xl = x[b]
        else:
            xl = np.concatenate([np.zeros((CT, D), np.float32), x[b, :S - CT]], axis=0)
        m = dict(params)
        m.update(tabs[half])
        m["x"] = np.ascontiguousarray(xl)
        in_maps.append(m)
    res = run_bass_kernel_spmd(nc, in_maps, core_ids=list(range(n_cores)))
    out = np.zeros((B, S, D), np.float32)
    for c in range(min(n_cores, 2 * B)):
        b, half = c // 2, c % 2
        o = res.results[c]["out"]
        if half == 0:
            out[b, :S - CT] = o
        else:
            out[b, CT:] = o
    return out


def kernel(**inputs):
    cfg = Cfg(mode="B")
    return run_model(cfg, inputs)
```
